# Optimizing a Trainium2 kernel written in Bass

```python
import jax, jax.numpy as jnp
from jax import lax
import numpy as np

D_MODEL = 1024
BATCH = 4
SEQ = 8192
DEPTH = 2

N_EVEN = (DEPTH + 1) // 2
N_ODD = DEPTH // 2
EPS = 1e-6

GLA_HEADS = 4
GLA_DK = 64
GLA_DV = 128
GLA_RANK = 16
GLA_TAU = 16.0
GLA_CHUNK = 64
GLA_QK = GLA_HEADS * GLA_DK
GLA_V = GLA_HEADS * GLA_DV

SWA_HEADS = 8
SWA_KV_HEADS = 2
SWA_HEAD_DIM = 64
SWA_GROUP = SWA_HEADS // SWA_KV_HEADS
WINDOW = 128
SWA_BLOCK = 128
SWA_Q = SWA_HEADS * SWA_HEAD_DIM
SWA_KV = SWA_KV_HEADS * SWA_HEAD_DIM

N_BUCKETS = 32
MAX_DISTANCE = 128

IN_SPLITS = (GLA_QK, GLA_QK, GLA_V, GLA_V, GLA_RANK, SWA_Q, SWA_KV, SWA_KV)
IN_COLS = GLA_QK + GLA_QK + GLA_V + GLA_V + GLA_RANK + SWA_Q + SWA_KV + SWA_KV
MIX_OUT = GLA_V + SWA_Q

POOL_WINDOWS = (2, 4, 8, 16)
POOL_GROUP = D_MODEL // 4

D_FF = 4 * D_MODEL

kernel_name = "hybrid_gla_swa_pool_adaln"


def rms_norm(x, w):
    xf = x.astype(jnp.float32)
    y = xf * lax.rsqrt(jnp.mean(xf * xf, axis=-1, keepdims=True) + EPS)
    return (y * w.astype(jnp.float32)).astype(x.dtype)


def ada_modulation(c, w, b):
    mod = jax.nn.silu(c) @ w + b
    shift, scale, gate = jnp.split(mod, 3, axis=-1)
    return shift[:, None], scale[:, None], gate[:, None]


def t5_bucket(dist):
    max_exact = N_BUCKETS // 2
    d = np.maximum(dist, 1).astype(np.float64)
    large = max_exact + (np.log(d / max_exact) / np.log(MAX_DISTANCE / max_exact)
                         * (N_BUCKETS - max_exact)).astype(np.int32)
    large = np.minimum(large, N_BUCKETS - 1)
    return np.where(dist < max_exact, dist, large).astype(np.int32)


def gla_chunked(q, k, v, log_a):
    B, S, H, dk = q.shape
    dv = v.shape[-1]
    C = GLA_CHUNK
    nc = S // C
    f32 = jnp.float32
    qf = (q.astype(f32) * dk ** -0.5).reshape(B, nc, C, H, dk)
    kf = k.astype(f32).reshape(B, nc, C, H, dk)
    vf = v.astype(f32).reshape(B, nc, C, H, dv)
    b = jnp.cumsum(log_a.reshape(B, nc, C, H, dk), axis=2)
    b_last = b[:, :, -1]
    q_t = qf * jnp.exp(b)
    k_t = kf * jnp.exp(-b)
    k_end = kf * jnp.exp(b_last[:, :, None] - b)
    causal = jnp.tril(jnp.ones((C, C), dtype=bool))
    a = jnp.einsum('bnihd,bnjhd->bnhij', q_t, k_t)
    a = jnp.where(causal, a, 0.0)
    o_intra = jnp.einsum('bnhij,bnjhe->bnihe', a, vf)
    d_state = jnp.einsum('bnjhd,bnjhe->bnhde', k_end, vf)
    decay = jnp.exp(b_last)

    def step(s_prev, inp):
        ds, dec = inp
        return dec[..., None] * s_prev + ds, s_prev

    _, s_in = lax.scan(step, jnp.zeros((B, H, dk, dv), f32),
                       (jnp.moveaxis(d_state, 1, 0), jnp.moveaxis(decay, 1, 0)))
    s_in = jnp.moveaxis(s_in, 0, 1)
    o_inter = jnp.einsum('bnihd,bnhde->bnihe', q_t, s_in)
    return (o_intra + o_inter).reshape(B, S, H, dv)


def sliding_window_attention(q, k, v, rel_bias, sinks):
    B, S = q.shape[0], q.shape[1]
    nb = S // SWA_BLOCK
    L = SWA_BLOCK
    f32 = jnp.float32
    qb = q.reshape(B, nb, L, SWA_KV_HEADS, SWA_GROUP, SWA_HEAD_DIM)

    def band(t):
        tb = t.reshape(B, nb, L, SWA_KV_HEADS, SWA_HEAD_DIM)
        prev = jnp.pad(tb, ((0, 0), (1, 0), (0, 0), (0, 0), (0, 0)))[:, :-1]
        return jnp.concatenate([prev, tb], axis=2)

    kb, vb = band(k), band(v)
    s = jnp.einsum('bnqhgd,bnkhd->bnhgqk', qb, kb,
                   preferred_element_type=f32) * (SWA_HEAD_DIM ** -0.5)
    qi = np.arange(L)[:, None] + L
    kj = np.arange(2 * L)[None, :]
    dist = qi - kj
    valid = (dist >= 0) & (dist < WINDOW)
    bucket = t5_bucket(np.maximum(dist, 0))
    bias = jnp.transpose(rel_bias[jnp.asarray(bucket)].astype(f32), (2, 0, 1))
    s = s + bias.reshape(SWA_KV_HEADS, SWA_GROUP, L, 2 * L)
    blk_ok = (np.arange(nb)[:, None, None] > 0) | (np.arange(2 * L)[None, None, :] >= L)
    mask = valid[None] & blk_ok
    s = jnp.where(jnp.asarray(mask)[None, :, None, None], s, -1e30)
    sink = sinks.astype(f32).reshape(SWA_KV_HEADS, SWA_GROUP)[None, None, :, :, None, None]
    m = jnp.maximum(jnp.max(s, axis=-1, keepdims=True), sink)
    p = jnp.exp(s - m)
    p = p / (jnp.sum(p, axis=-1, keepdims=True) + jnp.exp(sink - m))
    o = jnp.einsum('bnhgqk,bnkhd->bnqhgd', p, vb.astype(f32))
    return o.reshape(B, S, SWA_Q).astype(q.dtype)


def hybrid_attention_mixer(h, w_in, gla_w_gate, gla_b_gate, gla_norm_w, sinks, rel_bias, w_out):
    B, S, _ = h.shape
    proj = h @ w_in
    offs = np.cumsum(IN_SPLITS)[:-1].tolist()
    gq, gk, gv, gg, glr, sq, sk, sv = jnp.split(proj, offs, axis=-1)
    gate_logits = (glr @ gla_w_gate + gla_b_gate).astype(jnp.float32)
    log_a = (jax.nn.log_sigmoid(gate_logits) / GLA_TAU).reshape(B, S, GLA_HEADS, GLA_DK)
    o_gla = gla_chunked(gq.reshape(B, S, GLA_HEADS, GLA_DK), gk.reshape(B, S, GLA_HEADS, GLA_DK),
                        gv.reshape(B, S, GLA_HEADS, GLA_DV), log_a).astype(h.dtype)
    o_gla = rms_norm(o_gla, gla_norm_w) * jax.nn.silu(gg.reshape(B, S, GLA_HEADS, GLA_DV))
    o_gla = o_gla.reshape(B, S, GLA_V)
    o_swa = sliding_window_attention(sq.reshape(B, S, SWA_HEADS, SWA_HEAD_DIM),
                                     sk.reshape(B, S, SWA_KV_HEADS, SWA_HEAD_DIM),
                                     sv.reshape(B, S, SWA_KV_HEADS, SWA_HEAD_DIM), rel_bias, sinks)
    return jnp.concatenate([o_gla, o_swa], axis=-1) @ w_out


def multiscale_pool_mixer(h, pool_w, pool_scale):
    B, S, _ = h.shape
    hf = h.astype(jnp.float32)
    cs = jnp.cumsum(hf, axis=1)
    t = jnp.arange(S)
    outs = []
    for gi, w in enumerate(POOL_WINDOWS):
        lo, hi = gi * POOL_GROUP, (gi + 1) * POOL_GROUP
        csg = cs[..., lo:hi]
        lag = jnp.pad(csg, ((0, 0), (w, 0), (0, 0)))[:, :S]
        cnt = jnp.minimum(t + 1, w).astype(jnp.float32)[None, :, None]
        pooled = (csg - lag) / cnt - hf[..., lo:hi]
        outs.append(pooled.astype(h.dtype) @ pool_w[gi])
    return jnp.concatenate(outs, axis=-1) * pool_scale


def squared_relu_mlp(h, w1, w2):
    return jnp.square(jax.nn.relu(h @ w1)) @ w2


def setup_inputs(seed: int = 0) -> dict:
    key = jax.random.key(seed)
    ks = jax.random.split(key, 17)
    f32 = jnp.float32
    nrm = lambda k, s: jax.random.normal(k, s, f32)
    return {
        "x": nrm(ks[0], (BATCH, SEQ, D_MODEL)),
        "c": nrm(ks[1], (BATCH, D_MODEL)),
        "norm_w": 1.0 + 0.05 * nrm(ks[2], (DEPTH, 2, D_MODEL)),
        "ada_w": nrm(ks[3], (DEPTH, 2, D_MODEL, 3 * D_MODEL)) * (0.5 * D_MODEL ** -0.5),
        "ada_b": 0.01 * nrm(ks[4], (DEPTH, 2, 3 * D_MODEL)),
        "attn_w_in": nrm(ks[5], (N_EVEN, D_MODEL, IN_COLS)) * D_MODEL ** -0.5,
        "gla_w_gate": nrm(ks[6], (N_EVEN, GLA_RANK, GLA_QK)) * GLA_RANK ** -0.5,
        "gla_b_gate": 0.1 * nrm(ks[7], (N_EVEN, GLA_QK)),
        "gla_norm_w": 1.0 + 0.05 * nrm(ks[8], (N_EVEN, GLA_DV)),
        "attn_sinks": 0.5 * nrm(ks[9], (N_EVEN, SWA_HEADS)),
        "attn_w_out": nrm(ks[10], (N_EVEN, MIX_OUT, D_MODEL)) * MIX_OUT ** -0.5,
        "rel_bias": 0.5 * nrm(ks[11], (N_BUCKETS, SWA_HEADS)),
        "pool_w": nrm(ks[12], (N_ODD, 4, POOL_GROUP, POOL_GROUP)) * POOL_GROUP ** -0.5,
        "pool_scale": 1.0 + 0.1 * nrm(ks[13], (N_ODD, D_MODEL)),
        "mlp_w1": nrm(ks[14], (DEPTH, D_MODEL, D_FF)) * D_MODEL ** -0.5,
        "mlp_w2": nrm(ks[15], (DEPTH, D_FF, D_MODEL)) * D_FF ** -0.5,
        "final_norm_w": 1.0 + 0.05 * nrm(ks[16], (D_MODEL,)),
    }


def reference(x, c, norm_w, ada_w, ada_b, attn_w_in, gla_w_gate, gla_b_gate, gla_norm_w,
              attn_sinks, attn_w_out, rel_bias, pool_w, pool_scale, mlp_w1, mlp_w2, final_norm_w):
    for layer in range(DEPTH):
        i = layer // 2
        shift, scale, gate = ada_modulation(c, ada_w[layer, 0], ada_b[layer, 0])
        h = rms_norm(x, norm_w[layer, 0]) * (1.0 + scale) + shift
        if layer % 2 == 0:
            y = hybrid_attention_mixer(h, attn_w_in[i], gla_w_gate[i], gla_b_gate[i], gla_norm_w[i],
                                       attn_sinks[i], rel_bias, attn_w_out[i])
        else:
            y = multiscale_pool_mixer(h, pool_w[i], pool_scale[i])
        x = x + (gate * y).astype(x.dtype)
        shift, scale, gate = ada_modulation(c, ada_w[layer, 1], ada_b[layer, 1])
        h = rms_norm(x, norm_w[layer, 1]) * (1.0 + scale) + shift
        x = x + (gate * squared_relu_mlp(h, mlp_w1[layer], mlp_w2[layer])).astype(x.dtype)
    return rms_norm(x, final_norm_w)
```

```python
import contextlib
import numpy as np
import concourse.bass as bass
import concourse.mybir as mybir
from concourse.bass_utils import run_bass_kernel_spmd

F32 = mybir.dt.float32
BF16 = mybir.dt.bfloat16
AF = mybir.ActivationFunctionType
ALU = mybir.AluOpType

T = 512
EPS = 1e-6
NEG = -30000.0
NCONST = 225
C_NW, C_ADAB, C_FNW, C_PSC, C_BG, C_GNW, C_CB, C_FLAG, C_FM1, C_SINK, C_INVC = 0, 32, 128, 136, 144, 146, 147, 155, 156, 157, 161

COMPUTE = ("pe", "act", "dve", "pool")


class Op:
    __slots__ = ("eng", "fn", "deps", "signal", "sem", "val", "is_dma", "idx")

    def __init__(self, eng, fn, is_dma):
        self.eng, self.fn, self.is_dma = eng, fn, is_dma
        self.deps, self.signal, self.sem, self.val = [], False, None, None


class Sched:
    def __init__(self, nc, dma_slots=None, same_engine_sync=True):
        self.nc = nc
        self.queues = {"pe": [], "act": [], "dve": [], "pool": [], "sp": []}
        self.last_writer, self.readers = {}, {}
        self.dma_slots = dma_slots or {"sp": 12, "act": 2, "pool": 8}
        self.same_engine_sync = same_engine_sync

    def _add(self, eng, fn, reads, writes, is_dma):
        op = Op(eng, fn, is_dma)
        deps = set()
        for k in reads:
            w = self.last_writer.get(k)
            if w is not None:
                deps.add(w)
        for k in writes:
            w = self.last_writer.get(k)
            if w is not None:
                deps.add(w)
            for r in self.readers.get(k, ()):
                deps.add(r)
        for d in deps:
            if (not is_dma) and (not d.is_dma) and d.eng == eng:
                if eng == "pe" or not self.same_engine_sync:
                    continue
            op.deps.append(d)
            d.signal = True
        for k in writes:
            self.last_writer[k] = op
            self.readers[k] = []
        for k in reads:
            self.readers.setdefault(k, []).append(op)
        self.queues[eng].append(op)
        return op

    def op(self, eng, fn, reads=(), writes=()):
        return self._add(eng, fn, list(reads), list(writes), False)

    def dma(self, queue, fn, reads=(), writes=()):
        op = self._add(queue, fn, list(reads), list(writes), True)
        op.signal = True
        return op

    def emit(self, final_waits=()):
        nc = self.nc
        with contextlib.ExitStack() as st:
            csem = {e: st.enter_context(nc.semaphore("c_" + e)) for e in COMPUTE}
            dsem = {q: [st.enter_context(nc.semaphore(f"d_{q}{i}")) for i in range(n)]
                    for q, n in self.dma_slots.items()}
            for e in COMPUTE:
                c = 0
                for op in self.queues[e]:
                    if op.is_dma:
                        continue
                    if op.signal:
                        c += 1
                        op.sem, op.val = csem[e], c
            gate = {}
            for q, n in self.dma_slots.items():
                cnt, lastop, i = [0] * n, [None] * n, 0
                for op in self.queues[q]:
                    if not op.is_dma:
                        continue
                    s = i % n
                    cnt[s] += 1
                    op.sem, op.val = dsem[q][s], 16 * cnt[s]
                    if lastop[s] is not None:
                        gate[op] = lastop[s]
                    lastop[s] = op
                    i += 1
            block = st.enter_context(nc.Block())

            def make(ename):
                ops = self.queues[ename]

                def body(eng):
                    seen = {}

                    def wait(d):
                        key = id(d.sem)
                        if seen.get(key, 0) >= d.val:
                            return
                        eng.wait_ge(d.sem, d.val)
                        seen[key] = d.val

                    for op in ops:
                        if op in gate:
                            wait(gate[op])
                        for d in op.deps:
                            wait(d)
                        ins = op.fn(eng)
                        if op.signal:
                            ins.then_inc(op.sem, 16 if op.is_dma else 1)
                    if ename == "sp":
                        for d in final_waits:
                            wait(d)
                return body

            block.sync(make("sp"))
            block.tensor(make("pe"))
            block.scalar(make("act"))
            block.vector(make("dve"))
            block.gpsimd(make("pool"))


def t5_bucket(dist):
    n_buckets, max_distance = 32, 128
    max_exact = n_buckets // 2
    d = np.maximum(dist, 1).astype(np.float64)
    large = max_exact + (np.log(d / max_exact) / np.log(max_distance / max_exact)
                         * (n_buckets - max_exact)).astype(np.int32)
    large = np.minimum(large, n_buckets - 1)
    return np.where(dist < max_exact, dist, large).astype(np.int32)


def build(NW, NM, same_engine_sync=True, stop=None):
    nc = bass.Bass("TRN2", target_bir_lowering=False)

    def din(name, shape, dt=F32):
        return nc.dram_tensor(name, shape, dt, kind="ExternalInput").ap()

    def dscr(name, shape, dt):
        return nc.dram_tensor(name, shape, dt, kind="Internal")

    x_own = din("x_own", [NM * T, 1024])
    x_prev = din("x_prev", [NW * T, 1024])
    consts_d = din("consts", [128, NCONST])
    oh_d = din("oh", [32, 128])
    relb_d = din("rel_bias", [32, 8])
    ident_d = din("ident", [128, 128])
    cmask_d = din("cmask", [128, 512])
    jrev_d = din("jrev", [128, 128])
    w_in_d = din("w_in", [1024, 2320])
    w_out_d = din("w_out", [1024, 1024])
    w1_d = din("w1", [2, 1024, 4096])
    w2_d = din("w2", [2, 4096, 1024])
    wpool_d = din("pool_w", [4, 256, 256])
    ada_d = din("ada_w", [4, 1024, 3072])
    wg_d = din("w_gate", [16, 256])
    out_d = nc.dram_tensor("out", [NM * T, 1024], F32, kind="ExternalOutput").ap()

    wb_in = dscr("wb_in", [1024, 2320], BF16).ap()
    wb_out = dscr("wb_out", [1024, 1024], BF16).ap()
    wb1 = dscr("wb1", [2, 1024, 4096], BF16).ap()
    wb2 = dscr("wb2", [2, 4096, 1024], BF16).ap()
    wb_pool = dscr("wb_pool", [4, 256, 256], BF16).ap()
    wb_g = dscr("wb_g", [16, 256], BF16).ap()
    E_t = dscr("E_bias", [8, 384], F32)
    E_d = E_t.ap()

    S = Sched(nc, dma_slots={"sp": 12, "act": 2, "pool": 4}, same_engine_sync=same_engine_sync)

    with contextlib.ExitStack() as st:
        def sb(name, shape, dt=F32):
            return st.enter_context(nc.sbuf_tensor("s_" + name, shape, dt))

        def ps(name, shape, dt=F32):
            return st.enter_context(nc.psum_tensor("p_" + name, shape, dt))

        st.enter_context(nc.allow_non_contiguous_dma(reason="small one-off param layout DMAs"))
        st.enter_context(nc.allow_low_precision(reason="bf16 matmul operands, fp32 accumulate"))

        cst = sb("cst", [128, NCONST])
        ident = sb("ident", [128, 128])
        ident_b = sb("ident_b", [128, 128], BF16)
        ones_b = sb("ones_b", [128, 128], BF16)
        cmask = sb("cmask", [128, 512], BF16)
        rmask = sb("rmask", [128, 512])
        jrev = sb("jrev", [128, 128])
        oh_s = sb("oh_s", [32, 128])
        relb_s = sb("relb_s", [32, 8])
        bvs = sb("bvs", [128, 8])
        negt = sb("negt", [8, 128])
        scf = sb("scf", [128, 8])
        scb = sb("scb", [128, 8, 2], BF16)
        modrow = sb("modrow", [2, 512])
        onesf = sb("onesf", [1, 2])
        modv = sb("modv", [128, 96])
        sNv = sb("sNv", [128, 32])
        gps = sb("gps", [128, 8])
        negbg = sb("negbg", [128, 2])
        wg_s = sb("wg_s", [16, 256], BF16)
        wpool_s = sb("wpool_s", [128, 4, 2, 256], BF16)
        BTb = [sb(f"BTb{g}", [128, 2, 512], BF16) for g in range(2)]
        BT0b = [sb(f"BT0b{g}", [128, 512], BF16) for g in range(2)]
        escol = sb("escol", [128, 4])

        xs = [sb(f"xs{i}", [128, 1024]) for i in range(2)]
        xT = sb("xT", [128, 8, T])
        hT = sb("hT", [128, 8, T], BF16)
        hid = sb("hid", [128, 32, T], BF16)
        wbuf = [sb(f"wbuf{i}", [128, 8, 512], BF16) for i in range(4)]
        rsd = sb("rsd", [128, T])
        rs = sb("rs", [128, T])
        utmp = [sb(f"utmp{i}", [128, T]) for i in range(2)]
        vt = sb("vt", [128, 4, 640], BF16)
        vprev = sb("vprev", [128, 128], BF16)
        kTs = sb("kTs", [128, 640], BF16)
        qsz = [sb(f"qsz{i}", [128, 4, T], BF16) for i in range(2)]
        glrT = sb("glrT", [16, T], BF16)
        Bc = sb("Bc", [128, 2, T])
        wsA = sb("wsA", [128, 2, 528])
        wsB = sb("wsB", [128, 2, 528])
        eb = wsA[:, :, 0:T]
        enb = wsB[:, :, 0:T]
        dec = sb("dec", [128, 2, 8])
        declast = sb("declast", [128, 2])
        qgz = [sb(f"qgz{i}", [128, 2, T], BF16) for i in range(2)]
        kg = sb("kg", [128, 2, T], BF16)
        sg = sb("sg", [128, 4, T], BF16)
        ktok = sb("ktok", [128, 4, 2, 128], BF16)
        Uc = sb("Uc", [128, 2, 128])
        Uall = sb("Uall", [128, 8, 128])
        Lg = Uall[:].rearrange("p (a b) c -> p a (b c)", a=2)
        Sbf = sb("Sbf", [128, 2, 8, 128], BF16)
        Abf = sb("Abf", [128, 2, 512], BF16)
        og = sb("og", [128, 4, T], BF16)
        osw = sb("osw", [128, 4, T], BF16)
        sbt = sb("sbt", [128, 512])
        pT = [sb(f"pT{i}", [128, 512], BF16) for i in range(4)]
        hp = sb("hp", [128, 8, 528])

        pb = [ps(f"pb{i}", [128, 512]) for i in range(4)]
        pw = ps("pw", [128, 1024])
        p6 = ps("p6", [128, 512])
        pbt = ps("pbt", [128, 1024], BF16)

        gen_ctr = [0]

        def gen():
            i = gen_ctr[0] % 4
            gen_ctr[0] += 1
            return i

        wb_ctr = [0]

        def load_slab(src_ap, nk, ncols, reads):
            i = wb_ctr[0] % 4
            wb_ctr[0] += 1
            S.dma("sp", lambda e, i=i: e.dma_start(out=wbuf[i][:, 0:nk, 0:ncols],
                                                    in_=src_ap.rearrange("(kc p) n -> p kc n", p=128)),
                  reads=reads, writes=[("wbuf", i)])
            return i

        TS = {"s": slice(0, T), "n": T}
        HT = [("hT", k) for k in range(8)]
        XT = [("xT", k) for k in range(8)]
        HP = [("hp", k) for k in range(8)]

        S.dma("sp", lambda e: e.dma_start(out=cst[:], in_=consts_d), writes=["cst"])
        S.dma("sp", lambda e: e.dma_start(out=ident[:], in_=ident_d), writes=["ident"])
        S.dma("sp", lambda e: e.dma_start(out=sbt[:], in_=cmask_d), writes=["sbt"])
        S.dma("sp", lambda e: e.dma_start(out=jrev[:], in_=jrev_d), writes=["jrev"])
        S.dma("sp", lambda e: e.dma_start(out=oh_s[:], in_=oh_d), writes=["oh_s"])
        S.dma("sp", lambda e: e.dma_start(out=relb_s[:], in_=relb_d), writes=["relb_s"])

        def cast(dst, src, key):
            S.dma("pool", lambda e: e.dma_start(out=dst, in_=src), writes=[key])

        cast(wb_g, wg_d, "wb_g")
        cast(wb_in[:, 1536:1664], w_in_d[:, 2064:2192], "wb_in_d")
        cast(wb_in[:, 1664:1680], w_in_d[:, 1536:1552], "wb_in_e")
        cast(wb_in[:, 0:512], w_in_d[:, 0:512], "wb_in_a")
        cast(wb_in[:, 1680:2192], w_in_d[:, 512:1024], "wb_in_f")
        cast(wb_in[:, 512:1024], w_in_d[:, 1024:1536], "wb_in_b")
        for c in range(4):
            cast(wb_in[:, 1024 + c * 128:1024 + (c + 1) * 128].rearrange("r (g d) -> r g d", g=2),
                 w_in_d[:, 1552:2064].rearrange("r (g c d) -> r c g d", g=2, c=4, d=64)[:, c], "wb_in_c")
        cast(wb_in[:, 2192:2320], w_in_d[:, 2192:2320], "wb_in_g")
        WIN_KEYS = ["wb_in_a", "wb_in_b", "wb_in_c", "wb_in_d", "wb_in_e", "wb_in_f", "wb_in_g"]
        def cast_l0_items():
            items = [(wb_out[0:512, :], w_out_d[0:512, :], "wb_out_a")]
            for c in range(4):
                items.append((wb_out[512 + c * 128:512 + (c + 1) * 128, :].rearrange("(g d) n -> g d n", g=2),
                              w_out_d[512:1024, :].rearrange("(g c d) n -> c g d n", g=2, c=4, d=64)[c], "wb_out_b"))
            for r in range(4):
                items.append((wb1[0][r * 256:(r + 1) * 256, :], w1_d[0][r * 256:(r + 1) * 256, :], ("wb1_0", r)))
            for r in range(4):
                items.append((wb2[0][r * 1024:(r + 1) * 1024, :], w2_d[0][r * 1024:(r + 1) * 1024, :], ("wb2_0", r)))
            return items

        def cast_l1_items():
            items = [(wb_pool, wpool_d, "wb_pool")]
            for r in range(4):
                items.append((wb1[1][r * 256:(r + 1) * 256, :], w1_d[1][r * 256:(r + 1) * 256, :], ("wb1_1", r)))
            for r in range(4):
                items.append((wb2[1][r * 1024:(r + 1) * 1024, :], w2_d[1][r * 1024:(r + 1) * 1024, :], ("wb2_1", r)))
            return items

        def cast_after(items, tok):
            for dst, src, key in items:
                S.dma("pool", lambda e, dst=dst, src=src: e.dma_start(out=dst, in_=src), reads=[tok], writes=[key])

        S.dma("sp", lambda e: e.dma_start(out=wg_s[:], in_=wb_g), reads=["wb_g"], writes=["wg_s"])

        S.op("dve", lambda e: e.tensor_copy(out=ident_b[:], in_=ident[:]), reads=["ident"], writes=["ident_b"])
        S.op("dve", lambda e: e.tensor_copy(out=cmask[:], in_=sbt[:]), reads=["sbt"], writes=["cmask"])
        S.op("dve", lambda e: e.memset(ones_b[:], 1.0), writes=["ones_b"])
        S.op("dve", lambda e: e.memset(rmask[:], 1.0), writes=["rmask"])
        S.op("dve", lambda e: e.memset(rmask[:, 0::64], 0.0), writes=["rmask"])
        S.op("dve", lambda e: e.memset(negt[:], NEG), writes=["negt"])
        S.op("dve", lambda e: e.memset(Uc[:], 0.0), writes=["Uc"])
        S.op("dve", lambda e: e.memset(declast[:], 1.0), writes=["declast"])
        for i in range(2):
            S.op("dve", lambda e, i=i: e.memset(qgz[i][:], 0.0), writes=[("qg", 0), ("qg", 1)])
            S.op("dve", lambda e, i=i: e.memset(qsz[i][:], 0.0), writes=[("qs", c) for c in range(4)])
        S.op("dve", lambda e: e.memset(vprev[:], 0.0), writes=["vprev"])
        S.op("dve", lambda e: e.memset(kTs[:], 0.0), writes=["kTs_prev", "kTs_cur"])
        S.op("dve", lambda e: e.memset(hp[:], 0.0), writes=HP + ["hp_halo"])
        S.op("act", lambda e: e.activation(out=escol[:], in_=cst[:, C_SINK:C_SINK + 4], func=AF.Exp),
             reads=["cst"], writes=["escol"])

        S.op("act", lambda e: e.activation(out=scf[:], in_=cst[:, C_CB:C_CB + 8], func=AF.Silu),
             reads=["cst"], writes=["scf"])
        for dup in range(2):
            S.op("dve", lambda e, dup=dup: e.tensor_copy(out=scb[:, :, dup], in_=scf[:]), reads=["scf"], writes=["scb"])
        S.op("dve", lambda e: e.memset(onesf[:], 1.0), writes=["onesf"])
        ada_ctr = [0]

        def ada_piece(m, s3, hf, stage, skeys):
            S.dma("sp", lambda e: e.dma_start(
                out=stage, in_=ada_d[m][:, s3 * 1024 + hf * 512:s3 * 1024 + (hf + 1) * 512].rearrange(
                    "(kc p) n -> p kc n", p=128)), writes=skeys)
            slot = wb_ctr[0] % 4
            wb_ctr[0] += 1
            S.op("act", lambda e: e.activation(out=wbuf[slot][:], in_=stage, func=AF.Copy), reads=skeys,
                 writes=[("wbuf", slot)])
            col0 = m * 24 + s3 * 8 + hf * 4
            for kc in range(8):
                S.op("pe", lambda e, kc=kc: e.matmul(p6[0:2, :], scb[:, kc, :], wbuf[slot][:, kc, :],
                                                     start=(kc == 0), stop=(kc == 7)),
                     reads=[("wbuf", slot), "scb"], writes=["p6"])
            S.op("dve", lambda e: e.tensor_copy(out=modrow[:], in_=p6[0:2, :]), reads=["p6"], writes=["modrow"])
            for c4 in range(4):
                S.op("pe", lambda e, c4=c4: e.matmul(p6[:, 2 * c4:2 * c4 + 2], modrow[0:1, c4 * 128:(c4 + 1) * 128],
                                                     onesf[0:1, :], start=True, stop=True),
                     reads=["modrow", "onesf"], writes=["p6"])
            S.op("dve", lambda e: e.tensor_tensor(out=modv[:, col0:col0 + 4], in0=p6[:, 0:8:2],
                                                  in1=cst[:, C_ADAB + col0:C_ADAB + col0 + 4], op=ALU.add),
                 reads=["p6", "cst"], writes=[("modv", m)])

        def ada_mod(m, stages):
            for s3 in range(3):
                for hf in range(2):
                    stage, skeys = stages[ada_ctr[0] % len(stages)]
                    ada_ctr[0] += 1
                    ada_piece(m, s3, hf, stage, skeys)

        def ada_finish(m):
            S.op("dve", lambda e: e.scalar_tensor_tensor(
                out=sNv[:, m * 8:(m + 1) * 8], in0=modv[:, m * 24 + 8:m * 24 + 16], scalar=1.0,
                in1=cst[:, C_NW + m * 8:C_NW + (m + 1) * 8], op0=ALU.add, op1=ALU.mult),
                reads=[("modv", m), "cst"], writes=[("sNv", m)])
            if m == 2:
                S.op("dve", lambda e: e.tensor_tensor(out=gps[:], in0=modv[:, 2 * 24 + 16:2 * 24 + 24],
                                                      in1=cst[:, C_PSC:C_PSC + 8], op=ALU.mult),
                     reads=[("modv", 2), "cst"], writes=["gps"])

        ada_mod(0, [(xT[:], XT), (hp[:, :, 0:512], HP)])
        ada_finish(0)
        S.op("dve", lambda e: e.tensor_scalar(out=negbg[:], in0=cst[:, C_BG:C_BG + 2], scalar1=-1.0, scalar2=None,
                                              op0=ALU.mult), reads=["cst"], writes=["negbg"])

        def shift_ap(m, kc):
            return modv[:, m * 24 + kc:m * 24 + kc + 1]

        def gate_ap(m, kc):
            return modv[:, m * 24 + 16 + kc:m * 24 + 17 + kc]

        def late_prologue():
            S.op("pe", lambda e: e.matmul(pb[0][:, 0:8], oh_s[:], relb_s[:], start=True, stop=True),
                 reads=["oh_s", "relb_s"], writes=[("pb", 0)])
            S.op("dve", lambda e: e.tensor_copy(out=bvs[:], in_=pb[0][:, 0:8]), reads=[("pb", 0)], writes=["bvs"])
            S.dma("sp", lambda e: e.dma_start(out=E_d.rearrange("h u -> u h")[128:256, :], in_=bvs[:]),
                  reads=["bvs"], writes=["E"])
            S.dma("sp", lambda e: e.dma_start(out=E_d[:, 0:128], in_=negt[:]), reads=["negt"], writes=["E"])
            S.dma("sp", lambda e: e.dma_start(out=E_d[:, 256:384], in_=negt[:]), reads=["negt"], writes=["E"])
            if stop == "late1":
                return
            for g in range(2):
                for blk in range(2):
                    stg, stk = [(sbt, "sbt"), (rsd, "rsd"), (rs, "rs"), (utmp[0], ("utmp", 0))][g * 2 + blk]
                    for c in range(4):
                        off = (g * 4 + c) * 384 + 128 * (2 - blk) - 127
                        S.dma("sp", lambda e, c=c, off=off, stg=stg: e.dma_start(
                            out=stg[:, c * 128:(c + 1) * 128],
                            in_=bass.AP(tensor=E_t, offset=off, ap=[[1, 128], [1, 128]])),
                            reads=["E"], writes=[stk])
                    bH = gen()
                    S.op("pe", lambda e, bH=bH, stg=stg: e.matmul(pb[bH][:], jrev[:], stg[:], start=True, stop=True),
                         reads=["jrev", stk], writes=[("pb", bH)])
                    S.op("act", lambda e, bH=bH, g=g, blk=blk: e.activation(out=BTb[g][:, blk, :], in_=pb[bH][:],
                                                                            func=AF.Copy),
                         reads=[("pb", bH)], writes=[("BT", g)])
                    if blk == 0:
                        S.op("dve", lambda e, g=g: e.tensor_scalar(
                            out=BT0b[g][:], in0=BTb[g][:, 0, :], scalar1=cst[:, C_FM1:C_FM1 + 1], scalar2=None,
                            op0=ALU.add),
                            reads=[("BT", g), "cst"], writes=[("BT0", g)])

        late_prologue()

        def load_x(src, row0, tok=None):
            for blk in range(4):
                xb = xs[blk % 2]
                xk = ("xs", blk % 2)
                S.dma("sp", lambda e, xb=xb, blk=blk: e.dma_start(
                    out=xb[:], in_=src[row0 + blk * 128:row0 + (blk + 1) * 128, :]),
                    writes=[xk] + ([tok] if (tok is not None and blk == 0) else []))
                for half in range(2):
                    b = gen()
                    pv = pb[b][:].rearrange("p (j t) -> p j t", j=4)
                    for j in range(4):
                        kc = half * 4 + j
                        S.op("pe", lambda e, pv=pv, j=j, kc=kc, xb=xb: e.transpose(
                            pv[:, j, :], xb[:, kc * 128:(kc + 1) * 128], ident[:]),
                            reads=[xk, "ident"], writes=[("pb", b)])
                    S.op("act", lambda e, pv=pv, half=half, blk=blk: e.activation(
                        out=xT[:, half * 4:(half + 1) * 4, blk * 128:(blk + 1) * 128], in_=pv, func=AF.Copy),
                        reads=[("pb", b)], writes=[("xT", k) for k in range(half * 4, half * 4 + 4)])
                    S.op("act", lambda e, pv=pv, half=half, blk=blk: e.activation(
                        out=hT[:, half * 4:(half + 1) * 4, blk * 128:(blk + 1) * 128], in_=pv, func=AF.Square),
                        reads=[("pb", b)], writes=[("hT", k) for k in range(half * 4, half * 4 + 4)])

        def resid(d, psrc, pkeys, scal, skeys):
            ts = TS["s"]
            S.op("dve", lambda e: e.scalar_tensor_tensor(out=xT[:, d, ts], in0=psrc, scalar=scal, in1=xT[:, d, ts],
                                                         op0=ALU.mult, op1=ALU.add),
                 reads=pkeys + [("xT", d)] + skeys, writes=[("xT", d)])
            S.op("act", lambda e: e.activation(out=hT[:, d, ts], in_=xT[:, d, ts], func=AF.Square),
                 reads=[("xT", d)], writes=[("hT", d)])

        def norm(m, dest, dest_keys, sN_ap=None, use_shift=True):
            ts, n = TS["s"], TS["n"]
            for kc in range(8):
                S.op("pe", lambda e, kc=kc: e.matmul(p6[:, 0:n], ones_b[:], hT[:, kc, ts], start=(kc == 0), stop=(kc == 7)),
                     reads=[("hT", kc), "ones_b"], writes=["p6"])
            S.op("act", lambda e: e.activation(out=rsd[:, 0:n], in_=p6[:, 0:n], func=AF.Ln, scale=1.0 / 1024, bias=EPS),
                 reads=["p6"], writes=["rsd"])
            S.op("act", lambda e: e.activation(out=rs[:, 0:n], in_=rsd[:, 0:n], func=AF.Exp, scale=-0.5),
                 reads=["rsd"], writes=["rs"])
            mk = [("sNv", m), ("modv", m)] if m is not None else []
            for kc in range(8):
                ut = utmp[kc % 2]
                uk = ("utmp", kc % 2)
                S.op("dve", lambda e, kc=kc, ut=ut: e.tensor_tensor(out=ut[:, 0:n], in0=xT[:, kc, ts], in1=rs[:, 0:n],
                                                                    op=ALU.mult),
                     reads=[("xT", kc), "rs"], writes=[uk])
                sc = sN_ap(kc) if sN_ap is not None else sNv[:, m * 8 + kc:m * 8 + kc + 1]
                if use_shift:
                    S.op("act", lambda e, kc=kc, ut=ut, sc=sc: e.activation(
                        out=dest(kc, ts), in_=ut[:, 0:n], func=AF.Identity, scale=sc, bias=shift_ap(m, kc)),
                        reads=[uk, "cst"] + mk, writes=[dest_keys(kc)])
                else:
                    S.op("act", lambda e, kc=kc, ut=ut, sc=sc: e.activation(
                        out=dest(kc, ts), in_=ut[:, 0:n], func=AF.Copy, scale=sc),
                        reads=[uk, "cst"] + mk, writes=[dest_keys(kc)])

        def proj_fm(slot, colchunks, consumer):
            for ci, (c0, mcols) in enumerate(colchunks):
                b = gen()
                for kc in range(8):
                    S.op("pe", lambda e, b=b, kc=kc, c0=c0, mcols=mcols: e.matmul(
                        pb[b][0:mcols, :], wbuf[slot][:, kc, c0:c0 + mcols], hT[:, kc, :],
                        start=(kc == 0), stop=(kc == 7)),
                        reads=[("wbuf", slot), ("hT", kc)], writes=[("pb", b)])
                consumer(ci, b)

        def mlp(layer, m):
            ts, n = TS["s"], TS["n"]
            for sl in range(8):
                slot = load_slab(wb1[layer][:, sl * 512:(sl + 1) * 512], 8, 512, [(f"wb1_{layer}", r) for r in range(4)])
                for c in range(4):
                    ff = sl * 4 + c
                    b = gen()
                    for kc in range(8):
                        S.op("pe", lambda e, b=b, kc=kc, c=c, slot=slot: e.matmul(
                            pb[b][:, 0:n], wbuf[slot][:, kc, c * 128:(c + 1) * 128], hT[:, kc, ts],
                            start=(kc == 0), stop=(kc == 7)),
                            reads=[("wbuf", slot), ("hT", kc)], writes=[("pb", b)])
                    S.op("act", lambda e, b=b, ff=ff: e.activation(out=hid[:, ff, ts], in_=pb[b][:, 0:n], func=AF.Relu),
                         reads=[("pb", b)], writes=[("hid", ff)])
                    eng = "pool" if (ff % 2 == 0) else "dve"
                    S.op(eng, lambda e, ff=ff: e.tensor_tensor(out=hid[:, ff, ts], in0=hid[:, ff, ts], in1=hid[:, ff, ts],
                                                               op=ALU.mult),
                         reads=[("hid", ff)], writes=[("hid", ff)])
            for half in range(2):
                for fg in range(4):
                    slot = load_slab(wb2[layer][fg * 1024:(fg + 1) * 1024, half * 512:(half + 1) * 512], 8, 512,
                                     [(f"wb2_{layer}", fg)])
                    for dch in range(4):
                        for kc in range(8):
                            S.op("pe", lambda e, dch=dch, kc=kc, fg=fg, slot=slot: e.matmul(
                                pb[dch][:, 0:n], wbuf[slot][:, kc, dch * 128:(dch + 1) * 128], hid[:, fg * 8 + kc, ts],
                                start=(fg == 0 and kc == 0), stop=(fg == 3 and kc == 7)),
                                reads=[("wbuf", slot), ("hid", fg * 8 + kc)], writes=[("pb", dch)])
                for dch in range(4):
                    d = half * 4 + dch
                    resid(d, pb[dch][:, 0:n], [("pb", dch)], gate_ap(m, d), [("modv", m)])

        def gates(mode):
            slot = load_slab(wb_in[:, 1536:1680], 8, 144, WIN_KEYS)
            b = gen()
            for kc in range(8):
                S.op("pe", lambda e, b=b, kc=kc: e.matmul(pb[b][0:16, :], wbuf[slot][:, kc, 128:144], hT[:, kc, :],
                                                           start=(kc == 0), stop=(kc == 7)),
                     reads=[("wbuf", slot), ("hT", kc)], writes=[("pb", b)])
            S.op("act", lambda e, b=b: e.activation(out=glrT[:], in_=pb[b][0:16, :], func=AF.Copy),
                 reads=[("pb", b)], writes=["glrT"])
            for ch in range(2):
                b3 = gen()
                S.op("pe", lambda e, b3=b3, ch=ch: e.matmul(pb[b3][:], wg_s[0:16, ch * 128:(ch + 1) * 128], glrT[0:16, :],
                                                            start=True, stop=True),
                     reads=["wg_s", "glrT"], writes=[("pb", b3)])
                S.op("act", lambda e, b3=b3, ch=ch: e.activation(out=sbt[:], in_=pb[b3][:], func=AF.Exp, scale=-1.0,
                                                                 bias=negbg[:, ch:ch + 1]),
                     reads=[("pb", b3), "negbg"], writes=["sbt"])
                S.op("act", lambda e, ch=ch: e.activation(out=Lg[:, ch, :], in_=sbt[:], func=AF.Ln, bias=1.0),
                     reads=["sbt"], writes=["Uall"])
                S.op("dve", lambda e, ch=ch: e.tensor_tensor_scan(out=Bc[:, ch, :], data0=rmask[:], data1=Lg[:, ch, :],
                                                                  initial=0.0, op0=ALU.mult, op1=ALU.add),
                     reads=["Uall", "rmask"], writes=[("Bc", ch)])
                if mode != "state":
                    S.op("act", lambda e, ch=ch: e.activation(out=eb[:, ch, :], in_=Bc[:, ch, :], func=AF.Exp,
                                                              scale=-1.0 / 16), reads=[("Bc", ch)], writes=[("eb", ch)])
                S.op("act", lambda e, ch=ch: e.activation(out=enb[:, ch, :], in_=Bc[:, ch, :], func=AF.Exp,
                                                          scale=1.0 / 16), reads=[("Bc", ch)], writes=[("enb", ch)])
                S.op("act", lambda e, ch=ch: e.activation(out=dec[:, ch, :], in_=Bc[:, ch, 63::64], func=AF.Exp,
                                                          scale=-1.0 / 16), reads=[("Bc", ch)], writes=[("dec", ch)])
            return slot

        def inproj(mode):
            slot_k = gates(mode)
            if mode != "state":
                b2 = gen()
                for kc in range(8):
                    S.op("pe", lambda e, b2=b2, kc=kc: e.matmul(pb[b2][:], wbuf[slot_k][:, kc, 0:128], hT[:, kc, :],
                                                                 start=(kc == 0), stop=(kc == 7)),
                         reads=[("wbuf", slot_k), ("hT", kc)], writes=[("pb", b2)])
                S.op("act", lambda e, b2=b2: e.activation(out=kTs[:, 128:640], in_=pb[b2][:], func=AF.Copy),
                     reads=[("pb", b2)], writes=["kTs_cur"])
            slot_v = load_slab(wb_in[:, 1680:2192], 8, 512, WIN_KEYS)
            slot_s = load_slab(wb_in[:, 2192:2320], 8, 128, WIN_KEYS) if mode != "state" else None
            for blk in range(4):
                for kc in range(8):
                    S.op("pe", lambda e, blk=blk, kc=kc: e.matmul(
                        pw[:, 0:512], hT[:, kc, blk * 128:(blk + 1) * 128], wbuf[slot_v][:, kc, :],
                        start=(kc == 0), stop=(kc == 7)),
                        reads=[("wbuf", slot_v), ("hT", kc)], writes=["pw"])
                    if slot_s is not None:
                        S.op("pe", lambda e, blk=blk, kc=kc: e.matmul(
                            pw[:, 512:640], hT[:, kc, blk * 128:(blk + 1) * 128], wbuf[slot_s][:, kc, 0:128],
                            start=(kc == 0), stop=(kc == 7)),
                            reads=[("wbuf", slot_s), ("hT", kc)], writes=["pw"])
                ncol = 640 if slot_s is not None else 512
                S.op("act", lambda e, blk=blk, ncol=ncol: e.activation(out=vt[:, blk, 0:ncol], in_=pw[:, 0:ncol],
                                                                       func=AF.Copy),
                     reads=["pw"], writes=[("vt", blk)])
            if mode != "state":
                slot = load_slab(wb_in[:, 512:1024], 8, 512, WIN_KEYS)
                proj_fm(slot, [(c * 128, 128) for c in range(4)],
                        lambda ci, b: S.op("act", lambda e: e.activation(out=sg[:, ci, :], in_=pb[b][:], func=AF.Silu),
                                           reads=[("pb", b)], writes=[("sg", ci)]))
                slot = load_slab(wb_in[:, 1024:1536], 8, 512, WIN_KEYS)

                def cons_qs(ci, b):
                    for g in range(2):
                        gs = slice(g * 64, (g + 1) * 64)
                        S.op("act", lambda e, g=g, gs=gs: e.activation(out=qsz[g][gs, ci, :], in_=pb[b][gs, :],
                                                                      func=AF.Copy, scale=0.125),
                             reads=[("pb", b)], writes=[("qs", ci)])
                proj_fm(slot, [(c * 128, 128) for c in range(4)], cons_qs)
            slot = load_slab(wb_in[:, 0:512], 8, 512, WIN_KEYS)

            def cons_qk(ci, b):
                if ci < 2:
                    ch = ci
                    for hh in range(2):
                        hs = slice(hh * 64, (hh + 1) * 64)
                        S.op("dve", lambda e, hh=hh, hs=hs: e.scalar_tensor_tensor(
                            out=qgz[hh][hs, ch, :], in0=pb[b][hs, :], scalar=0.125, in1=eb[hs, ch, :],
                            op0=ALU.mult, op1=ALU.mult),
                            reads=[("pb", b), ("eb", ch)], writes=[("qg", ch)])
                else:
                    ch = ci - 2
                    S.op("dve", lambda e: e.tensor_tensor(out=kg[:, ch, :], in0=pb[b][:], in1=enb[:, ch, :], op=ALU.mult),
                         reads=[("pb", b), ("enb", ch)], writes=[("kg", ch)])

            if mode == "state":
                proj_fm(slot, [(256, 128), (384, 128)], lambda ci, b: cons_qk(2 + ci, b))
            else:
                proj_fm(slot, [(c * 128, 128) for c in range(4)], cons_qk)

        def gla_front():
            for blk in range(4):
                for ch in range(2):
                    i = blk * 2 + ch
                    S.op("pe", lambda e, blk=blk, ch=ch, i=i: e.transpose(
                        pbt[:, i * 128:(i + 1) * 128], kg[:, ch, blk * 128:(blk + 1) * 128], ident_b[:]),
                        reads=[("kg", ch), "ident_b"], writes=["pbt"])
            S.op("act", lambda e: e.activation(out=ktok[:].rearrange("p a b c -> p (a b c)"), in_=pbt[:], func=AF.Copy),
                 reads=["pbt"], writes=["ktok"])

        def gla_state(pr, mode):
            for n in range(8):
                blk, par = n // 2, n % 2
                for hh in range(2):
                    h = 2 * pr + hh
                    S.op("pe", lambda e, n=n, blk=blk, par=par, hh=hh, h=h: e.matmul(
                        pw[hh * 64:(hh + 1) * 64, (par * 4 + blk) * 128:(par * 4 + blk + 1) * 128],
                        ktok[par * 64:(par + 1) * 64, blk, pr, hh * 64:(hh + 1) * 64],
                        vt[par * 64:(par + 1) * 64, blk, h * 128:(h + 1) * 128], start=True, stop=True),
                        reads=["ktok", ("vt", blk)], writes=["pw"])
            for n in range(8):
                pcol = ((n % 2) * 4 + n // 2) * 128
                if n == 0:
                    S.op("dve", lambda e, pcol=pcol: e.scalar_tensor_tensor(
                        out=Uall[:, 0, :], in0=Uc[:, pr, :], scalar=declast[:, pr:pr + 1], in1=pw[:, pcol:pcol + 128],
                        op0=ALU.mult, op1=ALU.add),
                        reads=["Uc", "declast", "pw"], writes=["Uall"])
                else:
                    S.op("dve", lambda e, n=n, pcol=pcol: e.scalar_tensor_tensor(
                        out=Uall[:, n, :], in0=Uall[:, n - 1, :], scalar=dec[:, pr, n - 1:n], in1=pw[:, pcol:pcol + 128],
                        op0=ALU.mult, op1=ALU.add),
                        reads=["Uall", ("dec", pr), "pw"], writes=["Uall"])
            if mode != "state":
                S.op("act", lambda e: e.activation(out=Sbf[:, pr, 0, :], in_=Uc[:, pr, :], func=AF.Copy,
                                                   scale=declast[:, pr:pr + 1]),
                     reads=["Uc", "declast"], writes=[("Sbf", pr)])
                S.op("dve", lambda e: e.tensor_tensor(
                    out=Sbf[:, pr, 1:8, :], in0=Uall[:, 0:7, :],
                    in1=dec[:, pr, 0:7].unsqueeze(2).to_broadcast([128, 7, 128]), op=ALU.mult),
                    reads=["Uall", ("dec", pr)], writes=[("Sbf", pr)])
            S.op("dve", lambda e: e.tensor_copy(out=Uc[:, pr, :], in_=Uall[:, 7, :]), reads=["Uall"], writes=["Uc"])

        def gla_out(pr):
            bA = [gen(), gen()]
            for n in range(8):
                blk, par = n // 2, n % 2
                for hh in range(2):
                    S.op("pe", lambda e, n=n, blk=blk, par=par, hh=hh, bA=bA: e.matmul(
                        pb[bA[hh]][par * 64:(par + 1) * 64, (blk * 2 + par) * 64:(blk * 2 + par + 1) * 64],
                        kg[hh * 64:(hh + 1) * 64, pr, n * 64:(n + 1) * 64],
                        qgz[hh][hh * 64:(hh + 1) * 64, pr, n * 64:(n + 1) * 64], start=True, stop=True),
                        reads=[("kg", pr), ("qg", pr)], writes=[("pb", bA[hh])])
            for hh in range(2):
                S.op("dve", lambda e, bA=bA, hh=hh: e.tensor_tensor(out=Abf[:, hh, :], in0=pb[bA[hh]][:], in1=cmask[:],
                                                                    op=ALU.mult),
                     reads=[("pb", bA[hh]), "cmask"], writes=[("Abf", hh)])
            for hh in range(2):
                h = 2 * pr + hh
                bO = gen()
                for n in range(8):
                    blk, par = n // 2, n % 2
                    S.op("pe", lambda e, n=n, blk=blk, par=par, hh=hh, h=h, bO=bO: e.matmul(
                        pb[bO][:, n * 64:(n + 1) * 64], vt[:, blk, h * 128:(h + 1) * 128],
                        Abf[:, hh, (blk * 2 + par) * 64:(blk * 2 + par + 1) * 64],
                        start=True, stop=False),
                        reads=[("vt", blk), ("Abf", hh)], writes=[("pb", bO)])
                    S.op("pe", lambda e, n=n, hh=hh, bO=bO: e.matmul(
                        pb[bO][:, n * 64:(n + 1) * 64], Sbf[:, pr, n, :],
                        qgz[hh][:, pr, n * 64:(n + 1) * 64], start=False, stop=True),
                        reads=[("Sbf", pr), ("qg", pr)], writes=[("pb", bO)])
                sq = pT[hh]
                S.op("act", lambda e, bO=bO, sq=sq: e.activation(out=sq[:], in_=pb[bO][:], func=AF.Square),
                     reads=[("pb", bO)], writes=[("pT", hh)])
                S.op("pe", lambda e, sq=sq: e.matmul(p6[:], ones_b[:], sq[:], start=True, stop=True),
                     reads=[("pT", hh), "ones_b"], writes=["p6"])
                S.op("act", lambda e: e.activation(out=rsd[:], in_=p6[:], func=AF.Ln, scale=1.0 / 128, bias=EPS),
                     reads=["p6"], writes=["rsd"])
                S.op("act", lambda e: e.activation(out=rs[:], in_=rsd[:], func=AF.Exp, scale=-0.5),
                     reads=["rsd"], writes=["rs"])
                S.op("dve", lambda e, bO=bO: e.scalar_tensor_tensor(
                    out=sbt[:], in0=pb[bO][:], scalar=cst[:, C_GNW:C_GNW + 1], in1=rs[:], op0=ALU.mult, op1=ALU.mult),
                    reads=[("pb", bO), "rs", "cst"], writes=["sbt"])
                S.op("dve", lambda e, h=h: e.tensor_tensor(out=og[:, h, :], in0=sbt[:], in1=sg[:, h, :], op=ALU.mult),
                     reads=["sbt", ("sg", h)], writes=[("og", h)])

        def gla_end():
            S.op("dve", lambda e: e.tensor_copy(out=declast[:], in_=dec[:, :, 7]),
                 reads=[("dec", 0), ("dec", 1)], writes=["declast"])

        swa_ctr = [0]

        def swa_unit(qb, g, first_main):
            gs = slice(g * 64, (g + 1) * 64)
            u = swa_ctr[0] % 2
            swa_ctr[0] += 1
            pts = [pT[2 * u], pT[2 * u + 1]]
            ptk = [("pT", 2 * u), ("pT", 2 * u + 1)]
            for kb in range(2):
                bS = gen()
                pv = pb[bS][:].rearrange("p (c t) -> p c t", c=4)
                kcol = (qb + kb) * 128
                kkey = "kTs_prev" if kcol == 0 else "kTs_cur"
                S.op("pe", lambda e, pv=pv, kcol=kcol: e.matmul(
                    pv, kTs[:, kcol:kcol + 128], qsz[g][:, 0:4, qb * 128:(qb + 1) * 128], start=True, stop=False),
                    reads=[kkey] + [("qs", c) for c in range(4)], writes=[("pb", bS)])
                if kb == 0 and qb == 0 and first_main:
                    bt, btk = BT0b[g][:], ("BT0", g)
                else:
                    bt, btk = BTb[g][:, kb, :], ("BT", g)
                S.op("pe", lambda e, bS=bS, bt=bt: e.matmul(pb[bS][:], ident_b[:], bt, start=False, stop=True),
                     reads=["ident_b", btk], writes=[("pb", bS)])
                S.op("act", lambda e, kb=kb, bS=bS: e.activation(out=pts[kb][:], in_=pb[bS][:], func=AF.Exp),
                     reads=[("pb", bS)], writes=[ptk[kb]])
            bP = gen()
            for kb in range(2):
                if qb == 0 and kb == 0:
                    vsrc, vkey = vprev[:, :], "vprev"
                else:
                    vsrc, vkey = vt[:, qb - 1 + kb, 512:640], ("vt", qb - 1 + kb)
                S.op("pe", lambda e, vsrc=vsrc, kb=kb: e.matmul(pb[bP][:], vsrc, pts[kb][:],
                                                               start=(kb == 0), stop=(kb == 1)),
                     reads=[vkey, ptk[kb]], writes=[("pb", bP)])
            bD = gen()
            for kb in range(2):
                S.op("pe", lambda e, kb=kb: e.matmul(pb[bD][:], ones_b[:], pts[kb][:], start=(kb == 0), stop=(kb == 1)),
                     reads=["ones_b", ptk[kb]], writes=[("pb", bD)])
            for c in range(4):
                S.op("act", lambda e, c=c: e.activation(out=rsd[gs, c * 128:(c + 1) * 128],
                                                        in_=pb[bD][gs, c * 128:(c + 1) * 128], func=AF.Ln,
                                                        bias=escol[gs, c:c + 1]),
                     reads=[("pb", bD), "escol"], writes=["rsd"])
            S.op("act", lambda e: e.activation(out=rs[gs, :], in_=rsd[gs, :], func=AF.Exp, scale=-1.0),
                 reads=["rsd"], writes=["rs"])
            S.op("dve", lambda e: e.tensor_tensor(
                out=osw[gs, 0:4, qb * 128:(qb + 1) * 128],
                in0=pb[bP][gs, :].rearrange("p (c t) -> p c t", c=4),
                in1=rs[gs, :].rearrange("p (c t) -> p c t", c=4), op=ALU.mult),
                reads=[("pb", bP), "rs"], writes=[("osw", c) for c in range(4)])

        def swa_end():
            S.op("dve", lambda e: e.tensor_copy(out=kTs[:, 0:128], in_=kTs[:, 512:640]),
                 reads=["kTs_cur"], writes=["kTs_prev"])
            S.op("dve", lambda e: e.tensor_copy(out=vprev[:], in_=vt[:, 3, 512:640]), reads=[("vt", 3)], writes=["vprev"])

        def mixer0(mode, first_main):
            gla_front()
            if mode == "state":
                for pr in range(2):
                    gla_state(pr, mode)
                gla_end()
                return
            units = [(qb, g) for qb in range(4) for g in range(2)]
            gla_state(0, mode)
            for u in units[0:2]:
                swa_unit(u[0], u[1], first_main)
            gla_state(1, mode)
            for u in units[2:4]:
                swa_unit(u[0], u[1], first_main)
            gla_out(0)
            for u in units[4:6]:
                swa_unit(u[0], u[1], first_main)
            gla_out(1)
            for u in units[6:8]:
                swa_unit(u[0], u[1], first_main)
            gla_end()
            swa_end()

        def outproj():
            ts, n = TS["s"], TS["n"]
            for half in range(2):
                slot = load_slab(wb_out[:, half * 512:(half + 1) * 512], 8, 512, ["wb_out_a", "wb_out_b"])
                for dch in range(4):
                    d = half * 4 + dch
                    b = gen()
                    for kc in range(8):
                        rhs = og[:, kc, ts] if kc < 4 else osw[:, kc - 4, ts]
                        rk = ("og", kc) if kc < 4 else ("osw", kc - 4)
                        S.op("pe", lambda e, b=b, kc=kc, dch=dch, rhs=rhs, slot=slot: e.matmul(
                            pb[b][:, 0:n], wbuf[slot][:, kc, dch * 128:(dch + 1) * 128], rhs,
                            start=(kc == 0), stop=(kc == 7)),
                            reads=[("wbuf", slot), rk], writes=[("pb", b)])
                    resid(d, pb[b][:, 0:n], [("pb", b)], gate_ap(0, d), [("modv", 0)])

        def pool_mixer(first_main):
            for gi in range(4):
                c0 = 2 * gi
                hk = [("hp", c0), ("hp", c0 + 1), "hp_halo"]
                src = hp[:, c0:c0 + 2, :]
                cur, curk = src, hk
                bufs = [(wsA, [("eb", 0), ("eb", 1)]), (wsB, [("enb", 0), ("enb", 1)])]
                lo = 0
                for si in range(gi + 1):
                    sh = 1 << si
                    lo = lo + sh
                    dst, dk = bufs[si % 2]
                    S.op("dve", lambda e, dst=dst, cur=cur, lo=lo, sh=sh: e.tensor_tensor(
                        out=dst[:, :, lo:528], in0=cur[:, :, lo:528], in1=cur[:, :, lo - sh:528 - sh], op=ALU.add),
                        reads=curk, writes=dk)
                    cur, curk = dst, dk
                w = 2 << gi
                for j in range(2):
                    S.op("dve", lambda e, cur=cur, j=j, c0=c0, w=w: e.scalar_tensor_tensor(
                        out=hid[:, 8 + c0 + j, :], in0=cur[:, j, 16:528], scalar=1.0 / w, in1=hp[:, c0 + j, 16:528],
                        op0=ALU.mult, op1=ALU.subtract),
                        reads=curk + [("hp", c0 + j)], writes=[("hid", 8 + c0 + j)])
                    if first_main:
                        S.op("dve", lambda e, cur=cur, j=j, gi=gi: e.tensor_tensor(
                            out=sbt[:, 0:16], in0=cur[:, j, 16:32], in1=cst[:, C_INVC + gi * 16:C_INVC + (gi + 1) * 16],
                            op=ALU.mult), reads=curk + ["cst"], writes=["sbt"])
                        S.op("dve", lambda e, j=j, c0=c0: e.tensor_tensor(
                            out=hid[:, 8 + c0 + j, 0:16], in0=sbt[:, 0:16], in1=hp[:, c0 + j, 16:32], op=ALU.subtract),
                            reads=["sbt", ("hp", c0 + j)], writes=[("hid", 8 + c0 + j)])
                for mo in range(2):
                    d = c0 + mo
                    b = gen()
                    for kc in range(2):
                        S.op("pe", lambda e, b=b, kc=kc, gi=gi, mo=mo, c0=c0: e.matmul(
                            pb[b][:], wpool_s[:, gi, kc, mo * 128:(mo + 1) * 128], hid[:, 8 + c0 + kc, :],
                            start=(kc == 0), stop=(kc == 1)),
                            reads=["wpool_s", ("hid", 8 + c0 + kc)], writes=[("pb", b)])
                    resid(d, pb[b][:], [("pb", b)], gps[:, d:d + 1], ["gps"])

        def save_halo(with_flag):
            if with_flag:
                S.op("dve", lambda e: e.tensor_scalar(out=hp[:, :, 0:16], in0=hp[:, :, 512:528],
                                                      scalar1=cst[:, C_FLAG:C_FLAG + 1], scalar2=None, op0=ALU.mult),
                     reads=HP + ["cst"], writes=["hp_halo"])
            else:
                S.op("dve", lambda e: e.tensor_copy(out=hp[:, :, 0:16], in_=hp[:, :, 512:528]),
                     reads=HP, writes=["hp_halo"])

        out_ops = []
        so_ctr = [0]

        def store_out(row0):
            for blk in range(4):
                for half in range(2):
                    b = gen()
                    pv = pb[b][:].rearrange("p (j t) -> p j t", j=4)
                    for j in range(4):
                        kc = half * 4 + j
                        S.op("pe", lambda e, pv=pv, j=j, kc=kc, blk=blk: e.transpose(
                            pv[:, j, :], hp[:, kc, 16 + blk * 128:16 + (blk + 1) * 128], ident[:]),
                            reads=[("hp", kc), "ident"], writes=[("pb", b)])
                    si = so_ctr[0] % 2
                    so_ctr[0] += 1
                    ob = utmp[si]
                    S.op("dve", lambda e, b=b, ob=ob: e.tensor_copy(out=ob[:], in_=pb[b][:]),
                         reads=[("pb", b)], writes=[("utmp", si)])
                    o = S.dma("pool", lambda e, ob=ob, blk=blk, half=half: e.dma_start(
                        out=out_d[row0 + blk * 128:row0 + (blk + 1) * 128, half * 512:(half + 1) * 512], in_=ob[:]),
                        reads=[("utmp", si)], writes=["out"])
                    out_ops.append(o)

        def hT_dest(kc, ts):
            return hT[:, kc, ts]

        def hT_key(kc):
            return ("hT", kc)

        def hp_dest(kc, ts):
            return hp[:, kc, 16 + ts.start:16 + ts.stop]

        def hp_key(kc):
            return ("hp", kc)

        def tile(mode, src, row0, first_main=False, last_warm=False, out_row0=None, tok=None, casts=()):
            load_x(src, row0, tok)
            if tok is not None:
                cast_after(casts, tok)
            norm(0, hT_dest, hT_key)
            inproj(mode)
            mixer0(mode, first_main)
            if mode == "state":
                return
            if mode == "l0":
                TS["s"], TS["n"] = slice(384, 512), 128
            outproj()
            norm(1, hT_dest, hT_key)
            mlp(0, 1)
            norm(2, hp_dest, hp_key)
            if mode == "l0":
                save_halo(with_flag=last_warm)
                TS["s"], TS["n"] = slice(0, T), T
                return
            pool_mixer(first_main)
            save_halo(with_flag=False)
            norm(3, hT_dest, hT_key)
            mlp(1, 3)
            norm(None, hp_dest, hp_key, sN_ap=lambda kc: cst[:, C_FNW + kc:C_FNW + kc + 1], use_shift=False)
            store_out(out_row0)

        def schedule():
            rest = [(m, s3, hf) for m in range(1, 4) for s3 in range(3) for hf in range(2)]
            n_state = NW - 1
            per = (len(rest) + max(n_state, 1) - 1) // max(n_state, 1)
            l0_items = cast_l0_items()
            n_c = max(min(n_state, 4), 1)
            cper = (len(l0_items) + n_c - 1) // n_c
            l1_items = cast_l1_items()
            if stop == "pro":
                return
            for w in range(n_state):
                tile("state", x_prev, w * T, tok=("tok", w), casts=l0_items[w * cper:(w + 1) * cper])
                for (m, s3, hf) in rest[w * per:(w + 1) * per]:
                    ada_piece(m, s3, hf, hp[:, :, 0:512], HP)
            if n_state == 0:
                for (m, s3, hf) in rest:
                    ada_piece(m, s3, hf, hp[:, :, 0:512], HP)
                cast_after(l0_items, "cst")
            for m in range(1, 4):
                ada_finish(m)
            if stop == "state":
                return
            tile("l0", x_prev, (NW - 1) * T, last_warm=True, tok=("tok", "l0"), casts=l1_items[0:5])
            if stop == "l0":
                return
            S.dma("sp", lambda e: e.dma_start(out=wpool_s[:], in_=wb_pool.rearrange("g (kc p) n -> p g kc n", p=128)),
                  reads=["wb_pool"], writes=["wpool_s"])
            S.op("dve", lambda e: e.tensor_scalar(out=Uc[:], in0=Uc[:], scalar1=cst[:, C_FLAG:C_FLAG + 1],
                                                  scalar2=None, op0=ALU.mult),
                 reads=["Uc", "cst"], writes=["Uc"])
            for tI in range(NM):
                tile("full", x_own, tI * T, first_main=(tI == 0), out_row0=tI * T,
                     tok=(("tok", "m0") if tI == 0 else None), casts=(l1_items[5:] if tI == 0 else ()))

        schedule()
        S.emit(final_waits=out_ops)
    return nc


def host_consts(c_b, flag, norm_w, ada_b, final_norm_w, pool_scale, gla_b_gate, gla_norm_w, attn_sinks, first):
    def fm(v):
        v = np.asarray(v, np.float32)
        return v.reshape(-1, 128).T

    cst = np.zeros((128, NCONST), np.float32)
    for l in range(2):
        for j in range(2):
            m = l * 2 + j
            cst[:, C_NW + m * 8:C_NW + (m + 1) * 8] = fm(norm_w[l, j])
            for s3 in range(3):
                cst[:, C_ADAB + m * 24 + s3 * 8:C_ADAB + m * 24 + (s3 + 1) * 8] = fm(ada_b[l, j, s3 * 1024:(s3 + 1) * 1024])
    cst[:, C_FNW:C_FNW + 8] = fm(final_norm_w)
    cst[:, C_PSC:C_PSC + 8] = fm(pool_scale[0])
    cst[:, C_BG:C_BG + 2] = fm(gla_b_gate[0])
    cst[:, C_GNW] = np.asarray(gla_norm_w[0], np.float32)
    cst[:, C_CB:C_CB + 8] = fm(c_b)
    cst[:, C_FLAG] = flag
    cst[:, C_FM1] = (flag - 1.0) * 30000.0
    sk = np.asarray(attn_sinks[0], np.float32)
    for c in range(4):
        cst[0:64, C_SINK + c] = sk[c]
        cst[64:128, C_SINK + c] = sk[4 + c]
    for gi, w in enumerate((2, 4, 8, 16)):
        t = np.arange(16)
        cnt = np.minimum(t + 1, w) if first else np.full(16, w)
        cst[:, C_INVC + gi * 16:C_INVC + (gi + 1) * 16] = (1.0 / cnt.astype(np.float32))[None, :]
    return cst


def host_static():
    d = np.arange(128)
    bucket = t5_bucket(d)
    oh = np.zeros((32, 128), np.float32)
    oh[bucket, d] = 1.0
    p = np.arange(128)[:, None]
    i = np.arange(64)[None, :]
    cmask = np.zeros((128, 4, 2, 64), np.float32)
    for par in range(2):
        cmask[:, :, par, :] = (((p // 64) == par) & ((p % 64) <= i)).astype(np.float32)[:, None, :]
    cmask = cmask.reshape(128, 512)
    return oh, np.eye(128, dtype=np.float32), np.ascontiguousarray(cmask)


def make_in_maps(inputs, NW, NM, cores):
    f32 = lambda a: np.ascontiguousarray(np.asarray(a, dtype=np.float32))
    x = f32(inputs["x"])
    oh, ident, cmask = host_static()
    shared = {
        "oh": oh, "ident": ident, "cmask": cmask, "jrev": np.ascontiguousarray(ident[::-1]),
        "rel_bias": f32(inputs["rel_bias"]),
        "w_in": f32(inputs["attn_w_in"][0]),
        "w_out": f32(inputs["attn_w_out"][0]),
        "w1": f32(inputs["mlp_w1"]),
        "w2": f32(inputs["mlp_w2"]),
        "pool_w": f32(inputs["pool_w"][0]),
        "ada_w": f32(np.asarray(inputs["ada_w"]).reshape(4, 1024, 3072)),
        "w_gate": f32(inputs["gla_w_gate"][0]),
    }
    maps = []
    for (b, t0, first) in cores:
        m = dict(shared)
        m["x_own"] = np.ascontiguousarray(x[b, t0:t0 + NM * T])
        if first:
            m["x_prev"] = np.zeros((NW * T, 1024), np.float32)
        else:
            m["x_prev"] = np.ascontiguousarray(x[b, t0 - NW * T:t0])
        m["consts"] = host_consts(np.asarray(inputs["c"])[b], 0.0 if first else 1.0, np.asarray(inputs["norm_w"]),
                                  np.asarray(inputs["ada_b"]), inputs["final_norm_w"], np.asarray(inputs["pool_scale"]),
                                  np.asarray(inputs["gla_b_gate"]), np.asarray(inputs["gla_norm_w"]),
                                  np.asarray(inputs["attn_sinks"]), first)
        maps.append(m)
    return maps


def kernel(**inputs):
    B, SEQ, D = 4, 8192, 1024
    NM = NW = 8
    cores = []
    for b in range(B):
        cores.append((b, 0, True))
        cores.append((b, SEQ // 2, False))
    nc = build(NW, NM)
    in_maps = make_in_maps(inputs, NW, NM, cores)
    res = run_bass_kernel_spmd(nc, in_maps, core_ids=list(range(8)))
    out = np.empty((B, SEQ, D), np.float32)
    for i, (b, t0, _) in enumerate(cores):
        out[b, t0:t0 + NM * T] = np.asarray(res.results[i]["out"], dtype=np.float32)
    return out
```

```python
import contextlib
import numpy as np
import concourse.bass as bass
import concourse.mybir as mybir
from concourse.bass_utils import run_bass_kernel_spmd

F32 = mybir.dt.float32
BF16 = mybir.dt.bfloat16
AF = mybir.ActivationFunctionType
ALU = mybir.AluOpType

T = 512
EPS = 1e-6
NEG = -30000.0
NCONST = 225
C_NW, C_ADAB, C_FNW, C_PSC, C_BG, C_GNW, C_CB, C_FLAG, C_FM1, C_SINK, C_INVC = 0, 32, 128, 136, 144, 146, 147, 155, 156, 157, 161

COMPUTE = ("pe", "act", "dve", "pool")


class Op:
    __slots__ = ("eng", "fn", "deps", "signal", "sem", "val", "is_dma", "idx")

    def __init__(self, eng, fn, is_dma):
        self.eng, self.fn, self.is_dma = eng, fn, is_dma
        self.deps, self.signal, self.sem, self.val = [], False, None, None


class Sched:
    def __init__(self, nc, dma_slots=None, same_engine_sync=True):
        self.nc = nc
        self.queues = {"pe": [], "act": [], "dve": [], "pool": [], "sp": []}
        self.last_writer, self.readers = {}, {}
        self.dma_slots = dma_slots or {"sp": 12, "act": 2, "pool": 8}
        self.same_engine_sync = same_engine_sync

    def _add(self, eng, fn, reads, writes, is_dma):
        op = Op(eng, fn, is_dma)
        deps = set()
        for k in reads:
            w = self.last_writer.get(k)
            if w is not None:
                deps.add(w)
        for k in writes:
            w = self.last_writer.get(k)
            if w is not None:
                deps.add(w)
            for r in self.readers.get(k, ()):
                deps.add(r)
        for d in deps:
            if (not is_dma) and (not d.is_dma) and d.eng == eng:
                if eng == "pe" or not self.same_engine_sync:
                    continue
            op.deps.append(d)
            d.signal = True
        for k in writes:
            self.last_writer[k] = op
            self.readers[k] = []
        for k in reads:
            self.readers.setdefault(k, []).append(op)
        self.queues[eng].append(op)
        return op

    def op(self, eng, fn, reads=(), writes=()):
        return self._add(eng, fn, list(reads), list(writes), False)

    def dma(self, queue, fn, reads=(), writes=()):
        op = self._add(queue, fn, list(reads), list(writes), True)
        op.signal = True
        return op

    def emit(self, final_waits=()):
        nc = self.nc
        with contextlib.ExitStack() as st:
            csem = {e: st.enter_context(nc.semaphore("c_" + e)) for e in COMPUTE}
            dsem = {q: [st.enter_context(nc.semaphore(f"d_{q}{i}")) for i in range(n)]
                    for q, n in self.dma_slots.items()}
            for e in COMPUTE:
                c = 0
                for op in self.queues[e]:
                    if op.is_dma:
                        continue
                    if op.signal:
                        c += 1
                        op.sem, op.val = csem[e], c
            gate = {}
            for q, n in self.dma_slots.items():
                cnt, lastop, i = [0] * n, [None] * n, 0
                for op in self.queues[q]:
                    if not op.is_dma:
                        continue
                    s = i % n
                    cnt[s] += 1
                    op.sem, op.val = dsem[q][s], 16 * cnt[s]
                    if lastop[s] is not None:
                        gate[op] = lastop[s]
                    lastop[s] = op
                    i += 1
            block = st.enter_context(nc.Block())

            def make(ename):
                ops = self.queues[ename]

                def body(eng):
                    seen = {}

                    def wait(d):
                        key = id(d.sem)
                        if seen.get(key, 0) >= d.val:
                            return
                        eng.wait_ge(d.sem, d.val)
                        seen[key] = d.val

                    for op in ops:
                        if op in gate:
                            wait(gate[op])
                        for d in op.deps:
                            wait(d)
                        ins = op.fn(eng)
                        if op.signal:
                            ins.then_inc(op.sem, 16 if op.is_dma else 1)
                    if ename == "sp":
                        for d in final_waits:
                            wait(d)
                return body

            block.sync(make("sp"))
            block.tensor(make("pe"))
            block.scalar(make("act"))
            block.vector(make("dve"))
            block.gpsimd(make("pool"))


def t5_bucket(dist):
    n_buckets, max_distance = 32, 128
    max_exact = n_buckets // 2
    d = np.maximum(dist, 1).astype(np.float64)
    large = max_exact + (np.log(d / max_exact) / np.log(max_distance / max_exact)
                         * (n_buckets - max_exact)).astype(np.int32)
    large = np.minimum(large, n_buckets - 1)
    return np.where(dist < max_exact, dist, large).astype(np.int32)


def build(NW, NM, same_engine_sync=True, stop=None):
    nc = bass.Bass("TRN2", target_bir_lowering=False)

    def din(name, shape, dt=F32):
        return nc.dram_tensor(name, shape, dt, kind="ExternalInput").ap()

    def dscr(name, shape, dt):
        return nc.dram_tensor(name, shape, dt, kind="Internal")

    x_own = din("x_own", [NM * T, 1024])
    x_prev = din("x_prev", [NW * T, 1024])
    consts_d = din("consts", [128, NCONST])
    oh_d = din("oh", [32, 128])
    relb_d = din("rel_bias", [32, 8])
    ident_d = din("ident", [128, 128])
    cmask_d = din("cmask", [128, 512])
    jrev_d = din("jrev", [128, 128])
    w_in_d = din("w_in", [1024, 2320])
    w_out_d = din("w_out", [1024, 1024])
    w1_d = din("w1", [2, 1024, 4096])
    w2_d = din("w2", [2, 4096, 1024])
    wpool_d = din("pool_w", [4, 256, 256])
    ada_d = din("ada_w", [4, 1024, 3072])
    wg_d = din("w_gate", [16, 256])
    out_d = nc.dram_tensor("out", [NM * T, 1024], F32, kind="ExternalOutput").ap()

    wb_in = dscr("wb_in", [1024, 2320], BF16).ap()
    wb_out = dscr("wb_out", [1024, 1024], BF16).ap()
    wb1 = dscr("wb1", [2, 1024, 4096], BF16).ap()
    wb2 = dscr("wb2", [2, 4096, 1024], BF16).ap()
    wb_pool = dscr("wb_pool", [4, 256, 256], BF16).ap()
    wb_g = dscr("wb_g", [16, 256], BF16).ap()
    E_t = dscr("E_bias", [8, 384], F32)
    E_d = E_t.ap()

    S = Sched(nc, dma_slots={"sp": 12, "act": 2, "pool": 4}, same_engine_sync=same_engine_sync)

    with contextlib.ExitStack() as st:
        def sb(name, shape, dt=F32):
            return st.enter_context(nc.sbuf_tensor("s_" + name, shape, dt))

        def ps(name, shape, dt=F32):
            return st.enter_context(nc.psum_tensor("p_" + name, shape, dt))

        st.enter_context(nc.allow_non_contiguous_dma(reason="small one-off param layout DMAs"))
        st.enter_context(nc.allow_low_precision(reason="bf16 matmul operands, fp32 accumulate"))

        cst = sb("cst", [128, NCONST])
        ident = sb("ident", [128, 128])
        ident_b = sb("ident_b", [128, 128], BF16)
        ones_b = sb("ones_b", [128, 128], BF16)
        cmask = sb("cmask", [128, 512], BF16)
        rmask = sb("rmask", [128, 512])
        jrev = sb("jrev", [128, 128])
        oh_s = sb("oh_s", [32, 128])
        relb_s = sb("relb_s", [32, 8])
        bvs = sb("bvs", [128, 8])
        negt = sb("negt", [8, 128])
        scf = sb("scf", [128, 8])
        scb = sb("scb", [128, 8, 2], BF16)
        modrow = sb("modrow", [2, 512])
        onesf = sb("onesf", [1, 2])
        modv = sb("modv", [128, 96])
        sNv = sb("sNv", [128, 32])
        gps = sb("gps", [128, 8])
        negbg = sb("negbg", [128, 2])
        wg_s = sb("wg_s", [16, 256], BF16)
        wpool_s = sb("wpool_s", [128, 4, 2, 256], BF16)
        BTb = [sb(f"BTb{g}", [128, 2, 512], BF16) for g in range(2)]
        BT0b = [sb(f"BT0b{g}", [128, 512], BF16) for g in range(2)]
        escol = sb("escol", [128, 4])

        xs = [sb(f"xs{i}", [128, 1024]) for i in range(2)]
        xT = sb("xT", [128, 8, T])
        hT = sb("hT", [128, 8, T], BF16)
        hid = sb("hid", [128, 32, T], BF16)
        wbuf = [sb(f"wbuf{i}", [128, 8, 512], BF16) for i in range(4)]
        rsd = sb("rsd", [128, T])
        rs = sb("rs", [128, T])
        utmp = [sb(f"utmp{i}", [128, T]) for i in range(2)]
        vt = sb("vt", [128, 4, 640], BF16)
        vprev = sb("vprev", [128, 128], BF16)
        kTs = sb("kTs", [128, 640], BF16)
        qsz = [sb(f"qsz{i}", [128, 4, T], BF16) for i in range(2)]
        glrT = sb("glrT", [16, T], BF16)
        Bc = sb("Bc", [128, 2, T])
        wsA = sb("wsA", [128, 2, 528])
        wsB = sb("wsB", [128, 2, 528])
        eb = wsA[:, :, 0:T]
        enb = wsB[:, :, 0:T]
        dec = sb("dec", [128, 2, 8])
        declast = sb("declast", [128, 2])
        qgz = [sb(f"qgz{i}", [128, 2, T], BF16) for i in range(2)]
        kg = sb("kg", [128, 2, T], BF16)
        sg = sb("sg", [128, 4, T], BF16)
        ktok = sb("ktok", [128, 4, 2, 128], BF16)
        Uc = sb("Uc", [128, 2, 128])
        Uall = sb("Uall", [128, 8, 128])
        Lg = Uall[:].rearrange("p (a b) c -> p a (b c)", a=2)
        Sbf = sb("Sbf", [128, 2, 8, 128], BF16)
        Abf = sb("Abf", [128, 2, 512], BF16)
        og = sb("og", [128, 4, T], BF16)
        osw = sb("osw", [128, 4, T], BF16)
        sbt = sb("sbt", [128, 512])
        pT = [sb(f"pT{i}", [128, 512], BF16) for i in range(4)]
        hp = sb("hp", [128, 8, 528])

        pb = [ps(f"pb{i}", [128, 512]) for i in range(4)]
        pw = ps("pw", [128, 1024])
        p6 = ps("p6", [128, 512])
        pbt = ps("pbt", [128, 1024], BF16)

        gen_ctr = [0]

        def gen():
            i = gen_ctr[0] % 4
            gen_ctr[0] += 1
            return i

        wb_ctr = [0]

        def load_slab(src_ap, nk, ncols, reads):
            i = wb_ctr[0] % 4
            wb_ctr[0] += 1
            S.dma("sp", lambda e, i=i: e.dma_start(out=wbuf[i][:, 0:nk, 0:ncols],
                                                    in_=src_ap.rearrange("(kc p) n -> p kc n", p=128)),
                  reads=reads, writes=[("wbuf", i)])
            return i

        TS = {"s": slice(0, T), "n": T}
        HT = [("hT", k) for k in range(8)]
        XT = [("xT", k) for k in range(8)]
        HP = [("hp", k) for k in range(8)]

        S.dma("sp", lambda e: e.dma_start(out=cst[:], in_=consts_d), writes=["cst"])
        S.dma("sp", lambda e: e.dma_start(out=ident[:], in_=ident_d), writes=["ident"])
        S.dma("sp", lambda e: e.dma_start(out=sbt[:], in_=cmask_d), writes=["sbt"])
        S.dma("sp", lambda e: e.dma_start(out=jrev[:], in_=jrev_d), writes=["jrev"])
        S.dma("sp", lambda e: e.dma_start(out=oh_s[:], in_=oh_d), writes=["oh_s"])
        S.dma("sp", lambda e: e.dma_start(out=relb_s[:], in_=relb_d), writes=["relb_s"])

        def cast(dst, src, key):
            S.dma("pool", lambda e: e.dma_start(out=dst, in_=src), writes=[key])

        cast(wb_g, wg_d, "wb_g")
        cast(wb_in[:, 1536:1664], w_in_d[:, 2064:2192], "wb_in_d")
        cast(wb_in[:, 1664:1680], w_in_d[:, 1536:1552], "wb_in_e")
        cast(wb_in[:, 0:512], w_in_d[:, 0:512], "wb_in_a")
        cast(wb_in[:, 1680:2192], w_in_d[:, 512:1024], "wb_in_f")
        cast(wb_in[:, 512:1024], w_in_d[:, 1024:1536], "wb_in_b")
        for c in range(4):
            cast(wb_in[:, 1024 + c * 128:1024 + (c + 1) * 128].rearrange("r (g d) -> r g d", g=2),
                 w_in_d[:, 1552:2064].rearrange("r (g c d) -> r c g d", g=2, c=4, d=64)[:, c], "wb_in_c")
        cast(wb_in[:, 2192:2320], w_in_d[:, 2192:2320], "wb_in_g")
        WIN_KEYS = ["wb_in_a", "wb_in_b", "wb_in_c", "wb_in_d", "wb_in_e", "wb_in_f", "wb_in_g"]
        def cast_l0_items():
            items = [(wb_out[0:512, :], w_out_d[0:512, :], "wb_out_a")]
            for c in range(4):
                items.append((wb_out[512 + c * 128:512 + (c + 1) * 128, :].rearrange("(g d) n -> g d n", g=2),
                              w_out_d[512:1024, :].rearrange("(g c d) n -> c g d n", g=2, c=4, d=64)[c], "wb_out_b"))
            for r in range(4):
                items.append((wb1[0][r * 256:(r + 1) * 256, :], w1_d[0][r * 256:(r + 1) * 256, :], ("wb1_0", r)))
            for r in range(4):
                items.append((wb2[0][r * 1024:(r + 1) * 1024, :], w2_d[0][r * 1024:(r + 1) * 1024, :], ("wb2_0", r)))
            return items

        def cast_l1_items():
            items = [(wb_pool, wpool_d, "wb_pool")]
            for r in range(4):
                items.append((wb1[1][r * 256:(r + 1) * 256, :], w1_d[1][r * 256:(r + 1) * 256, :], ("wb1_1", r)))
            for r in range(4):
                items.append((wb2[1][r * 1024:(r + 1) * 1024, :], w2_d[1][r * 1024:(r + 1) * 1024, :], ("wb2_1", r)))
            return items

        def cast_after(items, tok):
            for dst, src, key in items:
                S.dma("pool", lambda e, dst=dst, src=src: e.dma_start(out=dst, in_=src), reads=[tok], writes=[key])

        S.dma("sp", lambda e: e.dma_start(out=wg_s[:], in_=wb_g), reads=["wb_g"], writes=["wg_s"])

        S.op("dve", lambda e: e.tensor_copy(out=ident_b[:], in_=ident[:]), reads=["ident"], writes=["ident_b"])
        S.op("dve", lambda e: e.tensor_copy(out=cmask[:], in_=sbt[:]), reads=["sbt"], writes=["cmask"])
        S.op("dve", lambda e: e.memset(ones_b[:], 1.0), writes=["ones_b"])
        S.op("dve", lambda e: e.memset(rmask[:], 1.0), writes=["rmask"])
        S.op("dve", lambda e: e.memset(rmask[:, 0::64], 0.0), writes=["rmask"])
        S.op("dve", lambda e: e.memset(negt[:], NEG), writes=["negt"])
        S.op("dve", lambda e: e.memset(Uc[:], 0.0), writes=["Uc"])
        S.op("dve", lambda e: e.memset(declast[:], 1.0), writes=["declast"])
        for i in range(2):
            S.op("dve", lambda e, i=i: e.memset(qgz[i][:], 0.0), writes=[("qg", 0), ("qg", 1)])
            S.op("dve", lambda e, i=i: e.memset(qsz[i][:], 0.0), writes=[("qs", c) for c in range(4)])
        S.op("dve", lambda e: e.memset(vprev[:], 0.0), writes=["vprev"])
        S.op("dve", lambda e: e.memset(kTs[:], 0.0), writes=["kTs_prev", "kTs_cur"])
        S.op("dve", lambda e: e.memset(hp[:], 0.0), writes=HP + ["hp_halo"])
        S.op("act", lambda e: e.activation(out=escol[:], in_=cst[:, C_SINK:C_SINK + 4], func=AF.Exp),
             reads=["cst"], writes=["escol"])

        S.op("act", lambda e: e.activation(out=scf[:], in_=cst[:, C_CB:C_CB + 8], func=AF.Silu),
             reads=["cst"], writes=["scf"])
        for dup in range(2):
            S.op("dve", lambda e, dup=dup: e.tensor_copy(out=scb[:, :, dup], in_=scf[:]), reads=["scf"], writes=["scb"])
        S.op("dve", lambda e: e.memset(onesf[:], 1.0), writes=["onesf"])
        ada_ctr = [0]

        def ada_piece(m, s3, hf, stage, skeys):
            S.dma("sp", lambda e: e.dma_start(
                out=stage, in_=ada_d[m][:, s3 * 1024 + hf * 512:s3 * 1024 + (hf + 1) * 512].rearrange(
                    "(kc p) n -> p kc n", p=128)), writes=skeys)
            slot = wb_ctr[0] % 4
            wb_ctr[0] += 1
            S.op("act", lambda e: e.activation(out=wbuf[slot][:], in_=stage, func=AF.Copy), reads=skeys,
                 writes=[("wbuf", slot)])
            col0 = m * 24 + s3 * 8 + hf * 4
            for kc in range(8):
                S.op("pe", lambda e, kc=kc: e.matmul(p6[0:2, :], scb[:, kc, :], wbuf[slot][:, kc, :],
                                                     start=(kc == 0), stop=(kc == 7)),
                     reads=[("wbuf", slot), "scb"], writes=["p6"])
            S.op("dve", lambda e: e.tensor_copy(out=modrow[:], in_=p6[0:2, :]), reads=["p6"], writes=["modrow"])
            for c4 in range(4):
                S.op("pe", lambda e, c4=c4: e.transpose(p6[:, 2 * c4:2 * c4 + 2], modrow[0:2, c4 * 128:(c4 + 1) * 128],
                                                        ident[0:2, 0:2]),
                     reads=["modrow", "ident"], writes=["p6"])
            S.op("dve", lambda e: e.tensor_tensor(out=modv[:, col0:col0 + 4], in0=p6[:, 0:8:2],
                                                  in1=cst[:, C_ADAB + col0:C_ADAB + col0 + 4], op=ALU.add),
                 reads=["p6", "cst"], writes=[("modv", m)])

        def ada_mod(m, stages):
            for s3 in range(3):
                for hf in range(2):
                    stage, skeys = stages[ada_ctr[0] % len(stages)]
                    ada_ctr[0] += 1
                    ada_piece(m, s3, hf, stage, skeys)

        def ada_finish(m):
            S.op("dve", lambda e: e.scalar_tensor_tensor(
                out=sNv[:, m * 8:(m + 1) * 8], in0=modv[:, m * 24 + 8:m * 24 + 16], scalar=1.0,
                in1=cst[:, C_NW + m * 8:C_NW + (m + 1) * 8], op0=ALU.add, op1=ALU.mult),
                reads=[("modv", m), "cst"], writes=[("sNv", m)])
            if m == 2:
                S.op("dve", lambda e: e.tensor_tensor(out=gps[:], in0=modv[:, 2 * 24 + 16:2 * 24 + 24],
                                                      in1=cst[:, C_PSC:C_PSC + 8], op=ALU.mult),
                     reads=[("modv", 2), "cst"], writes=["gps"])

        ada_mod(0, [(xT[:], XT), (hp[:, :, 0:512], HP)])
        ada_finish(0)
        S.op("dve", lambda e: e.tensor_scalar(out=negbg[:], in0=cst[:, C_BG:C_BG + 2], scalar1=-1.0, scalar2=None,
                                              op0=ALU.mult), reads=["cst"], writes=["negbg"])

        def shift_ap(m, kc):
            return modv[:, m * 24 + kc:m * 24 + kc + 1]

        def gate_ap(m, kc):
            return modv[:, m * 24 + 16 + kc:m * 24 + 17 + kc]

        def late_prologue():
            S.op("pe", lambda e: e.matmul(pb[0][:, 0:8], oh_s[:], relb_s[:], start=True, stop=True),
                 reads=["oh_s", "relb_s"], writes=[("pb", 0)])
            S.op("dve", lambda e: e.tensor_copy(out=bvs[:], in_=pb[0][:, 0:8]), reads=[("pb", 0)], writes=["bvs"])
            S.dma("sp", lambda e: e.dma_start(out=E_d.rearrange("h u -> u h")[128:256, :], in_=bvs[:]),
                  reads=["bvs"], writes=["E"])
            S.dma("sp", lambda e: e.dma_start(out=E_d[:, 0:128], in_=negt[:]), reads=["negt"], writes=["E"])
            S.dma("sp", lambda e: e.dma_start(out=E_d[:, 256:384], in_=negt[:]), reads=["negt"], writes=["E"])
            if stop == "late1":
                return
            for g in range(2):
                for blk in range(2):
                    stg, stk = [(sbt, "sbt"), (rsd, "rsd"), (rs, "rs"), (utmp[0], ("utmp", 0))][g * 2 + blk]
                    for c in range(4):
                        off = (g * 4 + c) * 384 + 128 * (2 - blk) - 127
                        S.dma("sp", lambda e, c=c, off=off, stg=stg: e.dma_start(
                            out=stg[:, c * 128:(c + 1) * 128],
                            in_=bass.AP(tensor=E_t, offset=off, ap=[[1, 128], [1, 128]])),
                            reads=["E"], writes=[stk])
                    bH = gen()
                    S.op("pe", lambda e, bH=bH, stg=stg: e.matmul(pb[bH][:], jrev[:], stg[:], start=True, stop=True),
                         reads=["jrev", stk], writes=[("pb", bH)])
                    S.op("act", lambda e, bH=bH, g=g, blk=blk: e.activation(out=BTb[g][:, blk, :], in_=pb[bH][:],
                                                                            func=AF.Copy),
                         reads=[("pb", bH)], writes=[("BT", g)])
                    if blk == 0:
                        S.op("dve", lambda e, g=g: e.tensor_scalar(
                            out=BT0b[g][:], in0=BTb[g][:, 0, :], scalar1=cst[:, C_FM1:C_FM1 + 1], scalar2=None,
                            op0=ALU.add),
                            reads=[("BT", g), "cst"], writes=[("BT0", g)])

        def load_x(src, row0, tok=None):
            for blk in range(4):
                xb = xs[blk % 2]
                xk = ("xs", blk % 2)
                S.dma("sp", lambda e, xb=xb, blk=blk: e.dma_start(
                    out=xb[:], in_=src[row0 + blk * 128:row0 + (blk + 1) * 128, :]),
                    writes=[xk] + ([tok] if (tok is not None and blk == 0) else []))
                for half in range(2):
                    b = gen()
                    pv = pb[b][:].rearrange("p (j t) -> p j t", j=4)
                    for j in range(4):
                        kc = half * 4 + j
                        S.op("pe", lambda e, pv=pv, j=j, kc=kc, xb=xb: e.transpose(
                            pv[:, j, :], xb[:, kc * 128:(kc + 1) * 128], ident[:]),
                            reads=[xk, "ident"], writes=[("pb", b)])
                    S.op("act", lambda e, pv=pv, half=half, blk=blk: e.activation(
                        out=xT[:, half * 4:(half + 1) * 4, blk * 128:(blk + 1) * 128], in_=pv, func=AF.Copy),
                        reads=[("pb", b)], writes=[("xT", k) for k in range(half * 4, half * 4 + 4)])
                    S.op("act", lambda e, pv=pv, half=half, blk=blk: e.activation(
                        out=hT[:, half * 4:(half + 1) * 4, blk * 128:(blk + 1) * 128], in_=pv, func=AF.Square),
                        reads=[("pb", b)], writes=[("hT", k) for k in range(half * 4, half * 4 + 4)])

        def resid(d, psrc, pkeys, scal, skeys):
            ts = TS["s"]
            S.op("dve", lambda e: e.scalar_tensor_tensor(out=xT[:, d, ts], in0=psrc, scalar=scal, in1=xT[:, d, ts],
                                                         op0=ALU.mult, op1=ALU.add),
                 reads=pkeys + [("xT", d)] + skeys, writes=[("xT", d)])
            S.op("act", lambda e: e.activation(out=hT[:, d, ts], in_=xT[:, d, ts], func=AF.Square),
                 reads=[("xT", d)], writes=[("hT", d)])

        def norm(m, dest, dest_keys, sN_ap=None, use_shift=True):
            ts, n = TS["s"], TS["n"]
            for kc in range(8):
                S.op("pe", lambda e, kc=kc: e.matmul(p6[:, 0:n], ones_b[:], hT[:, kc, ts], start=(kc == 0), stop=(kc == 7)),
                     reads=[("hT", kc), "ones_b"], writes=["p6"])
            S.op("act", lambda e: e.activation(out=rsd[:, 0:n], in_=p6[:, 0:n], func=AF.Ln, scale=1.0 / 1024, bias=EPS),
                 reads=["p6"], writes=["rsd"])
            S.op("act", lambda e: e.activation(out=rs[:, 0:n], in_=rsd[:, 0:n], func=AF.Exp, scale=-0.5),
                 reads=["rsd"], writes=["rs"])
            mk = [("sNv", m), ("modv", m)] if m is not None else []
            for kc in range(8):
                ut = utmp[kc % 2]
                uk = ("utmp", kc % 2)
                S.op("dve", lambda e, kc=kc, ut=ut: e.tensor_tensor(out=ut[:, 0:n], in0=xT[:, kc, ts], in1=rs[:, 0:n],
                                                                    op=ALU.mult),
                     reads=[("xT", kc), "rs"], writes=[uk])
                sc = sN_ap(kc) if sN_ap is not None else sNv[:, m * 8 + kc:m * 8 + kc + 1]
                if use_shift:
                    S.op("act", lambda e, kc=kc, ut=ut, sc=sc: e.activation(
                        out=dest(kc, ts), in_=ut[:, 0:n], func=AF.Identity, scale=sc, bias=shift_ap(m, kc)),
                        reads=[uk, "cst"] + mk, writes=[dest_keys(kc)])
                else:
                    S.op("act", lambda e, kc=kc, ut=ut, sc=sc: e.activation(
                        out=dest(kc, ts), in_=ut[:, 0:n], func=AF.Copy, scale=sc),
                        reads=[uk, "cst"] + mk, writes=[dest_keys(kc)])

        def proj_fm(slot, colchunks, consumer):
            for ci, (c0, mcols) in enumerate(colchunks):
                b = gen()
                for kc in range(8):
                    S.op("pe", lambda e, b=b, kc=kc, c0=c0, mcols=mcols: e.matmul(
                        pb[b][0:mcols, :], wbuf[slot][:, kc, c0:c0 + mcols], hT[:, kc, :],
                        start=(kc == 0), stop=(kc == 7)),
                        reads=[("wbuf", slot), ("hT", kc)], writes=[("pb", b)])
                consumer(ci, b)

        def mlp(layer, m):
            ts, n = TS["s"], TS["n"]
            for sl in range(8):
                slot = load_slab(wb1[layer][:, sl * 512:(sl + 1) * 512], 8, 512, [(f"wb1_{layer}", r) for r in range(4)])
                for c in range(4):
                    ff = sl * 4 + c
                    b = gen()
                    for kc in range(8):
                        S.op("pe", lambda e, b=b, kc=kc, c=c, slot=slot: e.matmul(
                            pb[b][:, 0:n], wbuf[slot][:, kc, c * 128:(c + 1) * 128], hT[:, kc, ts],
                            start=(kc == 0), stop=(kc == 7)),
                            reads=[("wbuf", slot), ("hT", kc)], writes=[("pb", b)])
                    S.op("act", lambda e, b=b, ff=ff: e.activation(out=hid[:, ff, ts], in_=pb[b][:, 0:n], func=AF.Relu),
                         reads=[("pb", b)], writes=[("hid", ff)])
                    eng = "pool" if (ff % 2 == 0) else "dve"
                    S.op(eng, lambda e, ff=ff: e.tensor_tensor(out=hid[:, ff, ts], in0=hid[:, ff, ts], in1=hid[:, ff, ts],
                                                               op=ALU.mult),
                         reads=[("hid", ff)], writes=[("hid", ff)])
            for half in range(2):
                for fg in range(4):
                    slot = load_slab(wb2[layer][fg * 1024:(fg + 1) * 1024, half * 512:(half + 1) * 512], 8, 512,
                                     [(f"wb2_{layer}", fg)])
                    for dch in range(4):
                        for kc in range(8):
                            S.op("pe", lambda e, dch=dch, kc=kc, fg=fg, slot=slot: e.matmul(
                                pb[dch][:, 0:n], wbuf[slot][:, kc, dch * 128:(dch + 1) * 128], hid[:, fg * 8 + kc, ts],
                                start=(fg == 0 and kc == 0), stop=(fg == 3 and kc == 7)),
                                reads=[("wbuf", slot), ("hid", fg * 8 + kc)], writes=[("pb", dch)])
                for dch in range(4):
                    d = half * 4 + dch
                    resid(d, pb[dch][:, 0:n], [("pb", dch)], gate_ap(m, d), [("modv", m)])

        def gates(mode):
            slot = load_slab(wb_in[:, 1536:1680], 8, 144, WIN_KEYS)
            b = gen()
            for kc in range(8):
                S.op("pe", lambda e, b=b, kc=kc: e.matmul(pb[b][0:16, :], wbuf[slot][:, kc, 128:144], hT[:, kc, :],
                                                           start=(kc == 0), stop=(kc == 7)),
                     reads=[("wbuf", slot), ("hT", kc)], writes=[("pb", b)])
            S.op("act", lambda e, b=b: e.activation(out=glrT[:], in_=pb[b][0:16, :], func=AF.Copy),
                 reads=[("pb", b)], writes=["glrT"])
            for ch in range(2):
                b3 = gen()
                S.op("pe", lambda e, b3=b3, ch=ch: e.matmul(pb[b3][:], wg_s[0:16, ch * 128:(ch + 1) * 128], glrT[0:16, :],
                                                            start=True, stop=True),
                     reads=["wg_s", "glrT"], writes=[("pb", b3)])
                S.op("act", lambda e, b3=b3, ch=ch: e.activation(out=sbt[:], in_=pb[b3][:], func=AF.Exp, scale=-1.0,
                                                                 bias=negbg[:, ch:ch + 1]),
                     reads=[("pb", b3), "negbg"], writes=["sbt"])
                S.op("act", lambda e, ch=ch: e.activation(out=Lg[:, ch, :], in_=sbt[:], func=AF.Ln, bias=1.0),
                     reads=["sbt"], writes=["Uall"])
                S.op("dve", lambda e, ch=ch: e.tensor_tensor_scan(out=Bc[:, ch, :], data0=rmask[:], data1=Lg[:, ch, :],
                                                                  initial=0.0, op0=ALU.mult, op1=ALU.add),
                     reads=["Uall", "rmask"], writes=[("Bc", ch)])
                if mode != "state":
                    S.op("act", lambda e, ch=ch: e.activation(out=eb[:, ch, :], in_=Bc[:, ch, :], func=AF.Exp,
                                                              scale=-1.0 / 16), reads=[("Bc", ch)], writes=[("eb", ch)])
                S.op("act", lambda e, ch=ch: e.activation(out=enb[:, ch, :], in_=Bc[:, ch, :], func=AF.Exp,
                                                          scale=1.0 / 16), reads=[("Bc", ch)], writes=[("enb", ch)])
                S.op("act", lambda e, ch=ch: e.activation(out=dec[:, ch, :], in_=Bc[:, ch, 63::64], func=AF.Exp,
                                                          scale=-1.0 / 16), reads=[("Bc", ch)], writes=[("dec", ch)])
            return slot

        def inproj(mode):
            slot_k = gates(mode)
            if mode != "state":
                b2 = gen()
                for kc in range(8):
                    S.op("pe", lambda e, b2=b2, kc=kc: e.matmul(pb[b2][:], wbuf[slot_k][:, kc, 0:128], hT[:, kc, :],
                                                                 start=(kc == 0), stop=(kc == 7)),
                         reads=[("wbuf", slot_k), ("hT", kc)], writes=[("pb", b2)])
                S.op("act", lambda e, b2=b2: e.activation(out=kTs[:, 128:640], in_=pb[b2][:], func=AF.Copy),
                     reads=[("pb", b2)], writes=["kTs_cur"])
            slot_v = load_slab(wb_in[:, 1680:2192], 8, 512, WIN_KEYS)
            slot_s = load_slab(wb_in[:, 2192:2320], 8, 128, WIN_KEYS) if mode != "state" else None
            for blk in range(4):
                for kc in range(8):
                    S.op("pe", lambda e, blk=blk, kc=kc: e.matmul(
                        pw[:, 0:512], hT[:, kc, blk * 128:(blk + 1) * 128], wbuf[slot_v][:, kc, :],
                        start=(kc == 0), stop=(kc == 7)),
                        reads=[("wbuf", slot_v), ("hT", kc)], writes=["pw"])
                    if slot_s is not None:
                        S.op("pe", lambda e, blk=blk, kc=kc: e.matmul(
                            pw[:, 512:640], hT[:, kc, blk * 128:(blk + 1) * 128], wbuf[slot_s][:, kc, 0:128],
                            start=(kc == 0), stop=(kc == 7)),
                            reads=[("wbuf", slot_s), ("hT", kc)], writes=["pw"])
                ncol = 640 if slot_s is not None else 512
                S.op("act", lambda e, blk=blk, ncol=ncol: e.activation(out=vt[:, blk, 0:ncol], in_=pw[:, 0:ncol],
                                                                       func=AF.Copy),
                     reads=["pw"], writes=[("vt", blk)])
            if mode != "state":
                slot = load_slab(wb_in[:, 512:1024], 8, 512, WIN_KEYS)
                proj_fm(slot, [(c * 128, 128) for c in range(4)],
                        lambda ci, b: S.op("act", lambda e: e.activation(out=sg[:, ci, :], in_=pb[b][:], func=AF.Silu),
                                           reads=[("pb", b)], writes=[("sg", ci)]))
                slot = load_slab(wb_in[:, 1024:1536], 8, 512, WIN_KEYS)

                def cons_qs(ci, b):
                    for g in range(2):
                        gs = slice(g * 64, (g + 1) * 64)
                        S.op("act", lambda e, g=g, gs=gs: e.activation(out=qsz[g][gs, ci, :], in_=pb[b][gs, :],
                                                                      func=AF.Copy, scale=0.125),
                             reads=[("pb", b)], writes=[("qs", ci)])
                proj_fm(slot, [(c * 128, 128) for c in range(4)], cons_qs)
            slot = load_slab(wb_in[:, 0:512], 8, 512, WIN_KEYS)

            def cons_qk(ci, b):
                if ci < 2:
                    ch = ci
                    for hh in range(2):
                        hs = slice(hh * 64, (hh + 1) * 64)
                        S.op("dve", lambda e, hh=hh, hs=hs: e.scalar_tensor_tensor(
                            out=qgz[hh][hs, ch, :], in0=pb[b][hs, :], scalar=0.125, in1=eb[hs, ch, :],
                            op0=ALU.mult, op1=ALU.mult),
                            reads=[("pb", b), ("eb", ch)], writes=[("qg", ch)])
                else:
                    ch = ci - 2
                    S.op("dve", lambda e: e.tensor_tensor(out=kg[:, ch, :], in0=pb[b][:], in1=enb[:, ch, :], op=ALU.mult),
                         reads=[("pb", b), ("enb", ch)], writes=[("kg", ch)])

            if mode == "state":
                proj_fm(slot, [(256, 128), (384, 128)], lambda ci, b: cons_qk(2 + ci, b))
            else:
                proj_fm(slot, [(c * 128, 128) for c in range(4)], cons_qk)

        def gla_front():
            for blk in range(4):
                for ch in range(2):
                    i = blk * 2 + ch
                    S.op("pe", lambda e, blk=blk, ch=ch, i=i: e.transpose(
                        pbt[:, i * 128:(i + 1) * 128], kg[:, ch, blk * 128:(blk + 1) * 128], ident_b[:]),
                        reads=[("kg", ch), "ident_b"], writes=["pbt"])
            S.op("act", lambda e: e.activation(out=ktok[:].rearrange("p a b c -> p (a b c)"), in_=pbt[:], func=AF.Copy),
                 reads=["pbt"], writes=["ktok"])

        def gla_state(pr, mode):
            for n in range(8):
                blk, par = n // 2, n % 2
                for hh in range(2):
                    h = 2 * pr + hh
                    S.op("pe", lambda e, n=n, blk=blk, par=par, hh=hh, h=h: e.matmul(
                        pw[hh * 64:(hh + 1) * 64, (par * 4 + blk) * 128:(par * 4 + blk + 1) * 128],
                        ktok[par * 64:(par + 1) * 64, blk, pr, hh * 64:(hh + 1) * 64],
                        vt[par * 64:(par + 1) * 64, blk, h * 128:(h + 1) * 128], start=True, stop=True),
                        reads=["ktok", ("vt", blk)], writes=["pw"])
            for n in range(8):
                pcol = ((n % 2) * 4 + n // 2) * 128
                if n == 0:
                    S.op("dve", lambda e, pcol=pcol: e.scalar_tensor_tensor(
                        out=Uall[:, 0, :], in0=Uc[:, pr, :], scalar=declast[:, pr:pr + 1], in1=pw[:, pcol:pcol + 128],
                        op0=ALU.mult, op1=ALU.add),
                        reads=["Uc", "declast", "pw"], writes=["Uall"])
                else:
                    S.op("dve", lambda e, n=n, pcol=pcol: e.scalar_tensor_tensor(
                        out=Uall[:, n, :], in0=Uall[:, n - 1, :], scalar=dec[:, pr, n - 1:n], in1=pw[:, pcol:pcol + 128],
                        op0=ALU.mult, op1=ALU.add),
                        reads=["Uall", ("dec", pr), "pw"], writes=["Uall"])
            if mode != "state":
                S.op("act", lambda e: e.activation(out=Sbf[:, pr, 0, :], in_=Uc[:, pr, :], func=AF.Copy,
                                                   scale=declast[:, pr:pr + 1]),
                     reads=["Uc", "declast"], writes=[("Sbf", pr)])
                S.op("dve", lambda e: e.tensor_tensor(
                    out=Sbf[:, pr, 1:8, :], in0=Uall[:, 0:7, :],
                    in1=dec[:, pr, 0:7].unsqueeze(2).to_broadcast([128, 7, 128]), op=ALU.mult),
                    reads=["Uall", ("dec", pr)], writes=[("Sbf", pr)])
            S.op("dve", lambda e: e.tensor_copy(out=Uc[:, pr, :], in_=Uall[:, 7, :]), reads=["Uall"], writes=["Uc"])

        def gla_out(pr):
            bA = [gen(), gen()]
            for n in range(8):
                blk, par = n // 2, n % 2
                for hh in range(2):
                    S.op("pe", lambda e, n=n, blk=blk, par=par, hh=hh, bA=bA: e.matmul(
                        pb[bA[hh]][par * 64:(par + 1) * 64, (blk * 2 + par) * 64:(blk * 2 + par + 1) * 64],
                        kg[hh * 64:(hh + 1) * 64, pr, n * 64:(n + 1) * 64],
                        qgz[hh][hh * 64:(hh + 1) * 64, pr, n * 64:(n + 1) * 64], start=True, stop=True),
                        reads=[("kg", pr), ("qg", pr)], writes=[("pb", bA[hh])])
            for hh in range(2):
                S.op("dve", lambda e, bA=bA, hh=hh: e.tensor_tensor(out=Abf[:, hh, :], in0=pb[bA[hh]][:], in1=cmask[:],
                                                                    op=ALU.mult),
                     reads=[("pb", bA[hh]), "cmask"], writes=[("Abf", hh)])
            for hh in range(2):
                h = 2 * pr + hh
                bO = gen()
                for n in range(8):
                    blk, par = n // 2, n % 2
                    S.op("pe", lambda e, n=n, blk=blk, par=par, hh=hh, h=h, bO=bO: e.matmul(
                        pb[bO][:, n * 64:(n + 1) * 64], vt[:, blk, h * 128:(h + 1) * 128],
                        Abf[:, hh, (blk * 2 + par) * 64:(blk * 2 + par + 1) * 64],
                        start=True, stop=False),
                        reads=[("vt", blk), ("Abf", hh)], writes=[("pb", bO)])
                    S.op("pe", lambda e, n=n, hh=hh, bO=bO: e.matmul(
                        pb[bO][:, n * 64:(n + 1) * 64], Sbf[:, pr, n, :],
                        qgz[hh][:, pr, n * 64:(n + 1) * 64], start=False, stop=True),
                        reads=[("Sbf", pr), ("qg", pr)], writes=[("pb", bO)])
                sq = pT[hh]
                S.op("act", lambda e, bO=bO, sq=sq: e.activation(out=sq[:], in_=pb[bO][:], func=AF.Square),
                     reads=[("pb", bO)], writes=[("pT", hh)])
                S.op("pe", lambda e, sq=sq: e.matmul(p6[:], ones_b[:], sq[:], start=True, stop=True),
                     reads=[("pT", hh), "ones_b"], writes=["p6"])
                S.op("act", lambda e: e.activation(out=rsd[:], in_=p6[:], func=AF.Ln, scale=1.0 / 128, bias=EPS),
                     reads=["p6"], writes=["rsd"])
                S.op("act", lambda e: e.activation(out=rs[:], in_=rsd[:], func=AF.Exp, scale=-0.5),
                     reads=["rsd"], writes=["rs"])
                S.op("dve", lambda e, bO=bO: e.scalar_tensor_tensor(
                    out=sbt[:], in0=pb[bO][:], scalar=cst[:, C_GNW:C_GNW + 1], in1=rs[:], op0=ALU.mult, op1=ALU.mult),
                    reads=[("pb", bO), "rs", "cst"], writes=["sbt"])
                S.op("dve", lambda e, h=h: e.tensor_tensor(out=og[:, h, :], in0=sbt[:], in1=sg[:, h, :], op=ALU.mult),
                     reads=["sbt", ("sg", h)], writes=[("og", h)])

        def gla_end():
            S.op("dve", lambda e: e.tensor_copy(out=declast[:], in_=dec[:, :, 7]),
                 reads=[("dec", 0), ("dec", 1)], writes=["declast"])

        swa_ctr = [0]

        def swa_unit(qb, g, first_main):
            gs = slice(g * 64, (g + 1) * 64)
            u = swa_ctr[0] % 2
            swa_ctr[0] += 1
            pts = [pT[2 * u], pT[2 * u + 1]]
            ptk = [("pT", 2 * u), ("pT", 2 * u + 1)]
            for kb in range(2):
                bS = gen()
                pv = pb[bS][:].rearrange("p (c t) -> p c t", c=4)
                kcol = (qb + kb) * 128
                kkey = "kTs_prev" if kcol == 0 else "kTs_cur"
                S.op("pe", lambda e, pv=pv, kcol=kcol: e.matmul(
                    pv, kTs[:, kcol:kcol + 128], qsz[g][:, 0:4, qb * 128:(qb + 1) * 128], start=True, stop=False),
                    reads=[kkey] + [("qs", c) for c in range(4)], writes=[("pb", bS)])
                if kb == 0 and qb == 0 and first_main:
                    bt, btk = BT0b[g][:], ("BT0", g)
                else:
                    bt, btk = BTb[g][:, kb, :], ("BT", g)
                S.op("pe", lambda e, bS=bS, bt=bt: e.matmul(pb[bS][:], ident_b[:], bt, start=False, stop=True),
                     reads=["ident_b", btk], writes=[("pb", bS)])
                S.op("act", lambda e, kb=kb, bS=bS: e.activation(out=pts[kb][:], in_=pb[bS][:], func=AF.Exp),
                     reads=[("pb", bS)], writes=[ptk[kb]])
            bP = gen()
            for kb in range(2):
                if qb == 0 and kb == 0:
                    vsrc, vkey = vprev[:, :], "vprev"
                else:
                    vsrc, vkey = vt[:, qb - 1 + kb, 512:640], ("vt", qb - 1 + kb)
                S.op("pe", lambda e, vsrc=vsrc, kb=kb: e.matmul(pb[bP][:], vsrc, pts[kb][:],
                                                               start=(kb == 0), stop=(kb == 1)),
                     reads=[vkey, ptk[kb]], writes=[("pb", bP)])
            bD = gen()
            for kb in range(2):
                S.op("pe", lambda e, kb=kb: e.matmul(pb[bD][:], ones_b[:], pts[kb][:], start=(kb == 0), stop=(kb == 1)),
                     reads=["ones_b", ptk[kb]], writes=[("pb", bD)])
            for c in range(4):
                S.op("act", lambda e, c=c: e.activation(out=rsd[gs, c * 128:(c + 1) * 128],
                                                        in_=pb[bD][gs, c * 128:(c + 1) * 128], func=AF.Ln,
                                                        bias=escol[gs, c:c + 1]),
                     reads=[("pb", bD), "escol"], writes=["rsd"])
            S.op("act", lambda e: e.activation(out=rs[gs, :], in_=rsd[gs, :], func=AF.Exp, scale=-1.0),
                 reads=["rsd"], writes=["rs"])
            S.op("dve", lambda e: e.tensor_tensor(
                out=osw[gs, 0:4, qb * 128:(qb + 1) * 128],
                in0=pb[bP][gs, :].rearrange("p (c t) -> p c t", c=4),
                in1=rs[gs, :].rearrange("p (c t) -> p c t", c=4), op=ALU.mult),
                reads=[("pb", bP), "rs"], writes=[("osw", c) for c in range(4)])

        def swa_end():
            S.op("dve", lambda e: e.tensor_copy(out=kTs[:, 0:128], in_=kTs[:, 512:640]),
                 reads=["kTs_cur"], writes=["kTs_prev"])
            S.op("dve", lambda e: e.tensor_copy(out=vprev[:], in_=vt[:, 3, 512:640]), reads=[("vt", 3)], writes=["vprev"])

        def mixer0(mode, first_main):
            gla_front()
            if mode == "state":
                for pr in range(2):
                    gla_state(pr, mode)
                gla_end()
                return
            units = [(qb, g) for qb in range(4) for g in range(2)]
            gla_state(0, mode)
            for u in units[0:2]:
                swa_unit(u[0], u[1], first_main)
            gla_state(1, mode)
            for u in units[2:4]:
                swa_unit(u[0], u[1], first_main)
            gla_out(0)
            for u in units[4:6]:
                swa_unit(u[0], u[1], first_main)
            gla_out(1)
            for u in units[6:8]:
                swa_unit(u[0], u[1], first_main)
            gla_end()
            swa_end()

        def outproj():
            ts, n = TS["s"], TS["n"]
            for half in range(2):
                slot = load_slab(wb_out[:, half * 512:(half + 1) * 512], 8, 512, ["wb_out_a", "wb_out_b"])
                for dch in range(4):
                    d = half * 4 + dch
                    b = gen()
                    for kc in range(8):
                        rhs = og[:, kc, ts] if kc < 4 else osw[:, kc - 4, ts]
                        rk = ("og", kc) if kc < 4 else ("osw", kc - 4)
                        S.op("pe", lambda e, b=b, kc=kc, dch=dch, rhs=rhs, slot=slot: e.matmul(
                            pb[b][:, 0:n], wbuf[slot][:, kc, dch * 128:(dch + 1) * 128], rhs,
                            start=(kc == 0), stop=(kc == 7)),
                            reads=[("wbuf", slot), rk], writes=[("pb", b)])
                    resid(d, pb[b][:, 0:n], [("pb", b)], gate_ap(0, d), [("modv", 0)])

        def pool_mixer(first_main):
            for gi in range(4):
                c0 = 2 * gi
                hk = [("hp", c0), ("hp", c0 + 1), "hp_halo"]
                src = hp[:, c0:c0 + 2, :]
                cur, curk = src, hk
                bufs = [(wsA, [("eb", 0), ("eb", 1)]), (wsB, [("enb", 0), ("enb", 1)])]
                lo = 0
                for si in range(gi + 1):
                    sh = 1 << si
                    lo = lo + sh
                    dst, dk = bufs[si % 2]
                    S.op("dve", lambda e, dst=dst, cur=cur, lo=lo, sh=sh: e.tensor_tensor(
                        out=dst[:, :, lo:528], in0=cur[:, :, lo:528], in1=cur[:, :, lo - sh:528 - sh], op=ALU.add),
                        reads=curk, writes=dk)
                    cur, curk = dst, dk
                w = 2 << gi
                for j in range(2):
                    S.op("dve", lambda e, cur=cur, j=j, c0=c0, w=w: e.scalar_tensor_tensor(
                        out=hid[:, 8 + c0 + j, :], in0=cur[:, j, 16:528], scalar=1.0 / w, in1=hp[:, c0 + j, 16:528],
                        op0=ALU.mult, op1=ALU.subtract),
                        reads=curk + [("hp", c0 + j)], writes=[("hid", 8 + c0 + j)])
                    if first_main:
                        S.op("dve", lambda e, cur=cur, j=j, gi=gi: e.tensor_tensor(
                            out=sbt[:, 0:16], in0=cur[:, j, 16:32], in1=cst[:, C_INVC + gi * 16:C_INVC + (gi + 1) * 16],
                            op=ALU.mult), reads=curk + ["cst"], writes=["sbt"])
                        S.op("dve", lambda e, j=j, c0=c0: e.tensor_tensor(
                            out=hid[:, 8 + c0 + j, 0:16], in0=sbt[:, 0:16], in1=hp[:, c0 + j, 16:32], op=ALU.subtract),
                            reads=["sbt", ("hp", c0 + j)], writes=[("hid", 8 + c0 + j)])
                for mo in range(2):
                    d = c0 + mo
                    b = gen()
                    for kc in range(2):
                        S.op("pe", lambda e, b=b, kc=kc, gi=gi, mo=mo, c0=c0: e.matmul(
                            pb[b][:], wpool_s[:, gi, kc, mo * 128:(mo + 1) * 128], hid[:, 8 + c0 + kc, :],
                            start=(kc == 0), stop=(kc == 1)),
                            reads=["wpool_s", ("hid", 8 + c0 + kc)], writes=[("pb", b)])
                    resid(d, pb[b][:], [("pb", b)], gps[:, d:d + 1], ["gps"])

        def save_halo(with_flag):
            if with_flag:
                S.op("dve", lambda e: e.tensor_scalar(out=hp[:, :, 0:16], in0=hp[:, :, 512:528],
                                                      scalar1=cst[:, C_FLAG:C_FLAG + 1], scalar2=None, op0=ALU.mult),
                     reads=HP + ["cst"], writes=["hp_halo"])
            else:
                S.op("dve", lambda e: e.tensor_copy(out=hp[:, :, 0:16], in_=hp[:, :, 512:528]),
                     reads=HP, writes=["hp_halo"])

        out_ops = []
        so_ctr = [0]

        def store_out(row0):
            for blk in range(4):
                for half in range(2):
                    b = gen()
                    pv = pb[b][:].rearrange("p (j t) -> p j t", j=4)
                    for j in range(4):
                        kc = half * 4 + j
                        S.op("pe", lambda e, pv=pv, j=j, kc=kc, blk=blk: e.transpose(
                            pv[:, j, :], hp[:, kc, 16 + blk * 128:16 + (blk + 1) * 128], ident[:]),
                            reads=[("hp", kc), "ident"], writes=[("pb", b)])
                    si = so_ctr[0] % 2
                    so_ctr[0] += 1
                    ob = utmp[si]
                    S.op("dve", lambda e, b=b, ob=ob: e.tensor_copy(out=ob[:], in_=pb[b][:]),
                         reads=[("pb", b)], writes=[("utmp", si)])
                    o = S.dma("pool", lambda e, ob=ob, blk=blk, half=half: e.dma_start(
                        out=out_d[row0 + blk * 128:row0 + (blk + 1) * 128, half * 512:(half + 1) * 512], in_=ob[:]),
                        reads=[("utmp", si)], writes=["out"])
                    out_ops.append(o)

        def hT_dest(kc, ts):
            return hT[:, kc, ts]

        def hT_key(kc):
            return ("hT", kc)

        def hp_dest(kc, ts):
            return hp[:, kc, 16 + ts.start:16 + ts.stop]

        def hp_key(kc):
            return ("hp", kc)

        def tile(mode, src, row0, first_main=False, last_warm=False, out_row0=None, tok=None, casts=()):
            load_x(src, row0, tok)
            if tok is not None:
                cast_after(casts, tok)
            norm(0, hT_dest, hT_key)
            inproj(mode)
            mixer0(mode, first_main)
            if mode == "state":
                return
            if mode == "l0":
                TS["s"], TS["n"] = slice(384, 512), 128
            outproj()
            norm(1, hT_dest, hT_key)
            mlp(0, 1)
            norm(2, hp_dest, hp_key)
            if mode == "l0":
                save_halo(with_flag=last_warm)
                TS["s"], TS["n"] = slice(0, T), T
                return
            pool_mixer(first_main)
            save_halo(with_flag=False)
            norm(3, hT_dest, hT_key)
            mlp(1, 3)
            norm(None, hp_dest, hp_key, sN_ap=lambda kc: cst[:, C_FNW + kc:C_FNW + kc + 1], use_shift=False)
            store_out(out_row0)

        def schedule():
            rest = [(m, s3, hf) for m in range(1, 4) for s3 in range(3) for hf in range(2)]
            n_state = NW - 1
            per = (len(rest) + max(n_state, 1) - 1) // max(n_state, 1)
            l0_items = cast_l0_items()
            n_c = max(min(n_state, 4), 1)
            cper = (len(l0_items) + n_c - 1) // n_c
            l1_items = cast_l1_items()
            if stop == "pro":
                return
            for w in range(n_state):
                tile("state", x_prev, w * T, tok=("tok", w), casts=l0_items[w * cper:(w + 1) * cper])
                for (m, s3, hf) in rest[w * per:(w + 1) * per]:
                    ada_piece(m, s3, hf, hp[:, :, 0:512], HP)
            if n_state == 0:
                for (m, s3, hf) in rest:
                    ada_piece(m, s3, hf, hp[:, :, 0:512], HP)
                cast_after(l0_items, "cst")
            for m in range(1, 4):
                ada_finish(m)
            if stop == "state":
                return
            late_prologue()
            tile("l0", x_prev, (NW - 1) * T, last_warm=True, tok=("tok", "l0"), casts=l1_items[0:5])
            if stop == "l0":
                return
            S.dma("sp", lambda e: e.dma_start(out=wpool_s[:], in_=wb_pool.rearrange("g (kc p) n -> p g kc n", p=128)),
                  reads=["wb_pool"], writes=["wpool_s"])
            S.op("dve", lambda e: e.tensor_scalar(out=Uc[:], in0=Uc[:], scalar1=cst[:, C_FLAG:C_FLAG + 1],
                                                  scalar2=None, op0=ALU.mult),
                 reads=["Uc", "cst"], writes=["Uc"])
            for tI in range(NM):
                tile("full", x_own, tI * T, first_main=(tI == 0), out_row0=tI * T,
                     tok=(("tok", "m0") if tI == 0 else None), casts=(l1_items[5:] if tI == 0 else ()))

        schedule()
        S.emit(final_waits=out_ops)
    return nc


def host_consts(c_b, flag, norm_w, ada_b, final_norm_w, pool_scale, gla_b_gate, gla_norm_w, attn_sinks, first):
    def fm(v):
        v = np.asarray(v, np.float32)
        return v.reshape(-1, 128).T

    cst = np.zeros((128, NCONST), np.float32)
    for l in range(2):
        for j in range(2):
            m = l * 2 + j
            cst[:, C_NW + m * 8:C_NW + (m + 1) * 8] = fm(norm_w[l, j])
            for s3 in range(3):
                cst[:, C_ADAB + m * 24 + s3 * 8:C_ADAB + m * 24 + (s3 + 1) * 8] = fm(ada_b[l, j, s3 * 1024:(s3 + 1) * 1024])
    cst[:, C_FNW:C_FNW + 8] = fm(final_norm_w)
    cst[:, C_PSC:C_PSC + 8] = fm(pool_scale[0])
    cst[:, C_BG:C_BG + 2] = fm(gla_b_gate[0])
    cst[:, C_GNW] = np.asarray(gla_norm_w[0], np.float32)
    cst[:, C_CB:C_CB + 8] = fm(c_b)
    cst[:, C_FLAG] = flag
    cst[:, C_FM1] = (flag - 1.0) * 30000.0
    sk = np.asarray(attn_sinks[0], np.float32)
    for c in range(4):
        cst[0:64, C_SINK + c] = sk[c]
        cst[64:128, C_SINK + c] = sk[4 + c]
    for gi, w in enumerate((2, 4, 8, 16)):
        t = np.arange(16)
        cnt = np.minimum(t + 1, w) if first else np.full(16, w)
        cst[:, C_INVC + gi * 16:C_INVC + (gi + 1) * 16] = (1.0 / cnt.astype(np.float32))[None, :]
    return cst


def host_static():
    d = np.arange(128)
    bucket = t5_bucket(d)
    oh = np.zeros((32, 128), np.float32)
    oh[bucket, d] = 1.0
    p = np.arange(128)[:, None]
    i = np.arange(64)[None, :]
    cmask = np.zeros((128, 4, 2, 64), np.float32)
    for par in range(2):
        cmask[:, :, par, :] = (((p // 64) == par) & ((p % 64) <= i)).astype(np.float32)[:, None, :]
    cmask = cmask.reshape(128, 512)
    return oh, np.eye(128, dtype=np.float32), np.ascontiguousarray(cmask)


def make_in_maps(inputs, NW, NM, cores):
    f32 = lambda a: np.ascontiguousarray(np.asarray(a, dtype=np.float32))
    x = f32(inputs["x"])
    oh, ident, cmask = host_static()
    shared = {
        "oh": oh, "ident": ident, "cmask": cmask, "jrev": np.ascontiguousarray(ident[::-1]),
        "rel_bias": f32(inputs["rel_bias"]),
        "w_in": f32(inputs["attn_w_in"][0]),
        "w_out": f32(inputs["attn_w_out"][0]),
        "w1": f32(inputs["mlp_w1"]),
        "w2": f32(inputs["mlp_w2"]),
        "pool_w": f32(inputs["pool_w"][0]),
        "ada_w": f32(np.asarray(inputs["ada_w"]).reshape(4, 1024, 3072)),
        "w_gate": f32(inputs["gla_w_gate"][0]),
    }
    maps = []
    for (b, t0, first) in cores:
        m = dict(shared)
        m["x_own"] = np.ascontiguousarray(x[b, t0:t0 + NM * T])
        if first:
            m["x_prev"] = np.zeros((NW * T, 1024), np.float32)
        else:
            m["x_prev"] = np.ascontiguousarray(x[b, t0 - NW * T:t0])
        m["consts"] = host_consts(np.asarray(inputs["c"])[b], 0.0 if first else 1.0, np.asarray(inputs["norm_w"]),
                                  np.asarray(inputs["ada_b"]), inputs["final_norm_w"], np.asarray(inputs["pool_scale"]),
                                  np.asarray(inputs["gla_b_gate"]), np.asarray(inputs["gla_norm_w"]),
                                  np.asarray(inputs["attn_sinks"]), first)
        maps.append(m)
    return maps


def kernel(**inputs):
    B, SEQ, D = 4, 8192, 1024
    NM = NW = 8
    cores = []
    for b in range(B):
        cores.append((b, 0, True))
        cores.append((b, SEQ // 2, False))
    nc = build(NW, NM)
    in_maps = make_in_maps(inputs, NW, NM, cores)
    res = run_bass_kernel_spmd(nc, in_maps, core_ids=list(range(8)))
    out = np.empty((B, SEQ, D), np.float32)
    for i, (b, t0, _) in enumerate(cores):
        out[b, t0:t0 + NM * T] = np.asarray(res.results[i]["out"], dtype=np.float32)
    return out
```
